# Optimizing a Trainium2 kernel written in Bass

```python
import jax
import jax.numpy as jnp
from jax import lax
import numpy as np

D_MODEL = 2048
BATCH = 2
SEQ = 16384
DEPTH = 1
DEC_BATCH = 32
DEC_SEQ = 64
PAST_LEN = 2048

CHUNK = 64
BLOCK = 16
HG_EXPAND = 128
HG_WIDTH = D_MODEL // 2
HG_HEADS = HG_WIDTH // HG_EXPAND
HG_DK = HG_EXPAND
HG_DV = HG_WIDTH // HG_HEADS
GLA_WIDTH = D_MODEL - HG_WIDTH
GLA_HEADS = 4
GLA_DV = GLA_WIDTH // GLA_HEADS
GLA_DK = GLA_DV // 2
GLA_RANK = 16
GLA_GATE_NORMALIZER = 16.0
D_FF = ((8 * D_MODEL + 3 * 256 - 1) // (3 * 256)) * 256
EPS = 1e-6
IN_SIZES = (HG_HEADS * HG_DK, HG_HEADS * HG_DK, HG_HEADS * HG_DV, HG_WIDTH,
            GLA_HEADS * GLA_DK, GLA_HEADS * GLA_DK, GLA_HEADS * GLA_DV, GLA_WIDTH, GLA_RANK)
N_IN = sum(IN_SIZES)

kernel_name = "hybrid_hgrn2_gla_streaming_step"


def rmsnorm(x, w):
    xf = x.astype(jnp.float32)
    y = xf * lax.rsqrt(jnp.mean(xf * xf, axis=-1, keepdims=True) + EPS)
    return (y * w.astype(jnp.float32)).astype(x.dtype)


def split_cols(p, sizes):
    outs, off = [], 0
    for s in sizes:
        outs.append(p[..., off:off + s])
        off += s
    return outs


def to_heads(t, n_heads):
    b, t_len, _ = t.shape
    return t.reshape(b, t_len, n_heads, -1).transpose(0, 2, 1, 3)


def head_norm(o, w, dtype):
    y = o * lax.rsqrt(jnp.mean(o * o, axis=-1, keepdims=True) + EPS) * w.astype(jnp.float32)
    b, h, t_len, d = y.shape
    return y.transpose(0, 2, 1, 3).reshape(b, t_len, h * d).astype(dtype)


def gated_linear_recurrence(q, k, v, log_a, s0):
    b_sz, h, t_len, _ = q.shape
    dv = v.shape[-1]
    pad = (-t_len) % BLOCK
    nb = (t_len + pad) // BLOCK

    def blocks(t):
        t = jnp.pad(t.astype(jnp.float32), ((0, 0), (0, 0), (0, pad), (0, 0)))
        return jnp.moveaxis(t.reshape(b_sz, h, nb, BLOCK, t.shape[-1]), 2, 0)

    mask = jnp.tril(jnp.ones((BLOCK, BLOCK), dtype=bool))[:, :, None]

    def step(S, inp):
        qb, kb, vb, ab = inp
        cum = jnp.cumsum(ab, axis=2)
        o_inter = jnp.einsum('bhtk,bhkv->bhtv', qb * jnp.exp(cum), S)
        diff = cum[:, :, :, None, :] - cum[:, :, None, :, :]
        decay = jnp.exp(jnp.where(mask, diff, -jnp.inf))
        scores = jnp.einsum('bhtk,bhtsk,bhsk->bhts', qb, decay, kb)
        o = o_inter + jnp.einsum('bhts,bhsv->bhtv', scores, vb)
        last = cum[:, :, -1:, :]
        S = (jnp.exp(last[:, :, 0, :])[..., None] * S
             + jnp.einsum('bhsk,bhsv->bhkv', kb * jnp.exp(last - cum), vb))
        return S, o

    s_fin, o = lax.scan(step, s0.astype(jnp.float32), (blocks(q), blocks(k), blocks(v), blocks(log_a)))
    o = jnp.moveaxis(o, 0, 2).reshape(b_sz, h, nb * BLOCK, dv)[:, :, :t_len]
    return o, s_fin.astype(s0.dtype)


def mixer(h, s_hg, s_gla, lb, w_in, w_gk2, b_gk, hg_norm, gla_norm, w_out):
    p = h @ w_in
    hq, hf, hi, hgate, gq, gk, gv, ggate, gr = split_cols(p, IN_SIZES)
    zf = hf.astype(jnp.float32)
    lbf = lb.astype(jnp.float32)
    log_f = jnp.logaddexp(jnp.log(lbf), jnp.log1p(-lbf) + jax.nn.log_sigmoid(zf))
    k_hg = (1.0 - lbf) * jax.nn.sigmoid(-zf)
    q_hg = jax.nn.silu(hq.astype(jnp.float32)) * (HG_DK ** -0.5)
    o_hg, s_hg_new = gated_linear_recurrence(to_heads(q_hg, HG_HEADS), to_heads(k_hg, HG_HEADS),
                                             to_heads(hi, HG_HEADS), to_heads(log_f, HG_HEADS), s_hg)
    log_a = jax.nn.log_sigmoid((gr @ w_gk2 + b_gk).astype(jnp.float32)) / GLA_GATE_NORMALIZER
    q_g = gq.astype(jnp.float32) * (GLA_DK ** -0.5)
    o_gla, s_gla_new = gated_linear_recurrence(to_heads(q_g, GLA_HEADS), to_heads(gk, GLA_HEADS),
                                               to_heads(gv, GLA_HEADS), to_heads(log_a, GLA_HEADS), s_gla)
    o = jnp.concatenate([head_norm(o_hg, hg_norm, h.dtype) * jax.nn.silu(hgate),
                         head_norm(o_gla, gla_norm, h.dtype) * jax.nn.silu(ggate)], axis=-1)
    return o @ w_out, s_hg_new, s_gla_new


def swiglu(h, w_gate_up, w_down):
    g, u = jnp.split(h @ w_gate_up, 2, axis=-1)
    return (jax.nn.silu(g) * u) @ w_down


def trunk(x, s_hg, s_gla, lb_logits, norm_mix, w_in, w_gk2, b_gk, hg_norm, gla_norm, w_out,
          norm_ffn, w_gate_up, w_down, norm_final):
    lb_all = jnp.cumsum(jax.nn.softmax(lb_logits.astype(jnp.float32), axis=0), axis=0)[:DEPTH]
    new_hg, new_gla = [], []
    for l in range(DEPTH):
        m, a, b = mixer(rmsnorm(x, norm_mix[l]), s_hg[l], s_gla[l], lb_all[l], w_in[l], w_gk2[l], b_gk[l],
                        hg_norm[l], gla_norm[l], w_out[l])
        x = x + m
        x = x + swiglu(rmsnorm(x, norm_ffn[l]), w_gate_up[l], w_down[l])
        new_hg.append(a)
        new_gla.append(b)
    return rmsnorm(x, norm_final), jnp.stack(new_hg), jnp.stack(new_gla)


def setup_inputs(seed: int = 0) -> dict:
    key = jax.random.key(seed)
    ks = jax.random.split(key, 20)
    f32 = jnp.float32
    nrm = lambda k, shape, s: jax.random.normal(k, shape, f32) * s
    return {
        'x_prompt': nrm(ks[0], (BATCH, SEQ, D_MODEL), 1.0),
        'x_sample': nrm(ks[1], (DEC_BATCH, DEC_SEQ, D_MODEL), 1.0),
        'state_hgrn': nrm(ks[2], (DEPTH, DEC_BATCH, HG_HEADS, HG_DK, HG_DV), 0.5),
        'state_gla': nrm(ks[3], (DEPTH, DEC_BATCH, GLA_HEADS, GLA_DK, GLA_DV), 1.0),
        'lb_logits': nrm(ks[4], (DEPTH + 1, HG_HEADS * HG_DK), 0.1),
        'norm_mix': 1.0 + nrm(ks[5], (DEPTH, D_MODEL), 0.02),
        'w_in': nrm(ks[6], (DEPTH, D_MODEL, N_IN), D_MODEL ** -0.5),
        'w_gk2': nrm(ks[7], (DEPTH, GLA_RANK, GLA_HEADS * GLA_DK), GLA_RANK ** -0.5),
        'b_gk': nrm(ks[8], (DEPTH, GLA_HEADS * GLA_DK), 0.02),
        'hg_norm': 1.0 + nrm(ks[9], (DEPTH, HG_DV), 0.02),
        'gla_norm': 1.0 + nrm(ks[10], (DEPTH, GLA_DV), 0.02),
        'w_out': nrm(ks[11], (DEPTH, D_MODEL, D_MODEL), D_MODEL ** -0.5),
        'norm_ffn': 1.0 + nrm(ks[12], (DEPTH, D_MODEL), 0.02),
        'w_gate_up': nrm(ks[13], (DEPTH, D_MODEL, 2 * D_FF), D_MODEL ** -0.5),
        'w_down': nrm(ks[14], (DEPTH, D_FF, D_MODEL), D_FF ** -0.5),
        'norm_final': 1.0 + nrm(ks[15], (D_MODEL,), 0.02),
    }


def reference(x_prompt, x_sample, state_hgrn, state_gla, lb_logits, norm_mix, w_in, w_gk2, b_gk,
              hg_norm, gla_norm, w_out, norm_ffn, w_gate_up, w_down, norm_final):
    b_p = x_prompt.shape[0]
    zero_hg = jnp.zeros((DEPTH, b_p, HG_HEADS, HG_DK, HG_DV), x_prompt.dtype)
    zero_gla = jnp.zeros((DEPTH, b_p, GLA_HEADS, GLA_DK, GLA_DV), x_prompt.dtype)
    y_prompt, st_hg_p, st_gla_p = trunk(x_prompt, zero_hg, zero_gla, lb_logits, norm_mix, w_in, w_gk2, b_gk,
                                        hg_norm, gla_norm, w_out, norm_ffn, w_gate_up, w_down, norm_final)
    y_sample, st_hg_s, st_gla_s = trunk(x_sample, state_hgrn, state_gla, lb_logits, norm_mix, w_in, w_gk2, b_gk,
                                        hg_norm, gla_norm, w_out, norm_ffn, w_gate_up, w_down, norm_final)
    return (y_prompt, y_sample, st_hg_p, st_gla_p, st_hg_s, st_gla_s)
```

```python
import contextlib
import os
DBG = os.environ.get('KDBG', '')
import math
import numpy as np
import ml_dtypes
import concourse.bass as bass
import concourse.mybir as mybir
from concourse.bass_utils import run_bass_kernel_spmd

F32 = mybir.dt.float32
BF16 = mybir.dt.bfloat16
AF = mybir.ActivationFunctionType
ALU = mybir.AluOpType

D = 2048
KC = 16
NIN = 7184
DFF = 5632
EPS = 1e-6
CH = 64
LNCQ = math.log(128.0 ** -0.5)
NDUM = int(os.environ.get('KNDUM', '14'))

O_HQ, O_HF, O_HI, O_HGATE, O_GQ, O_GK, O_GV, O_GGATE, O_GR = 0, 1024, 2048, 3072, 4096, 4608, 5120, 6144, 7168


class Op:
    __slots__ = ("eng", "fn", "deps", "dsem", "idx", "waits", "marked", "count", "dcount", "dinc")


class Sched:
    def __init__(self):
        self.ops = []
        self.last_w = {}
        self.readers = {}

    def op(self, eng, fn, reads=(), writes=(), dsem=None, dinc=16):
        o = Op()
        o.eng, o.fn, o.dsem, o.idx = eng, fn, dsem, len(self.ops)
        o.dinc = dinc
        o.marked = False
        deps = set()
        for k in reads:
            w = self.last_w.get(k)
            if w is not None:
                deps.add(w)
        for k in writes:
            w = self.last_w.get(k)
            if w is not None:
                deps.add(w)
            r = self.readers.get(k)
            if r:
                deps.update(r)
        for k in reads:
            rl = self.readers.setdefault(k, [])
            if dsem is None:
                for j in range(len(rl)):
                    if rl[j].dsem is None and rl[j].eng == eng:
                        rl[j] = o
                        break
                else:
                    rl.append(o)
            else:
                rl.append(o)
        for k in writes:
            self.last_w[k] = o
            self.readers[k] = []
        deps.discard(o)
        o.deps = deps
        self.ops.append(o)
        return o

    def finalize(self):
        for o in self.ops:
            for d in o.deps:
                if d.dsem is None:
                    if d.eng == o.eng and d.eng == "pe" and o.dsem is None:
                        continue
                    d.marked = True
        cnt = {}
        dcnt = {}
        dma_hist = {}
        for o in self.ops:
            if o.dsem is not None:
                dcnt[o.dsem] = dcnt.get(o.dsem, 0) + o.dinc
                o.dcount = dcnt[o.dsem]
                dma_hist.setdefault(o.dsem, []).append((o.idx, o.dcount))
            elif o.marked:
                cnt[o.eng] = cnt.get(o.eng, 0) + 1
                o.count = cnt[o.eng]
        waited = {}
        import bisect
        for o in self.ops:
            need = {}
            for d in o.deps:
                if d.dsem is not None:
                    hist = dma_hist[d.dsem]
                    j = bisect.bisect_left(hist, (o.idx, -1)) - 1
                    val = hist[j][1]
                    key = ("d", d.dsem)
                else:
                    if d.eng == o.eng and d.eng == "pe" and o.dsem is None:
                        continue
                    val = d.count
                    key = ("e", d.eng)
                if val > need.get(key, 0):
                    need[key] = val
            ws = []
            for key, val in need.items():
                wk = (o.eng, key)
                if waited.get(wk, 0) >= val:
                    continue
                waited[wk] = val
                ws.append((key, val))
            o.waits = ws


class Kern:
    def __init__(self, nblk_p=8, nseq_s=4):
        self.nblk_p = nblk_p
        self.nseq_s = nseq_s
        self.TP = nblk_p * 512
        self.TS = nseq_s * 64
        self.s = Sched()
        self.rings = {}

    def ring(self, name, n):
        v = self.rings.get(name, 0)
        self.rings[name] = v + 1
        return v % n

    def dram_in(self, name, shape, dt=F32):
        return self.nc.dram_tensor(name, list(shape), dt, kind="ExternalInput").ap()

    def dram_out(self, name, shape, dt=F32):
        return self.nc.dram_tensor(name, list(shape), dt, kind="ExternalOutput").ap()

    def sb(self, name, shape, dt):
        return self.es.enter_context(self.nc.sbuf_tensor(name, list(shape), dt))

    def ps(self, name, shape, dt):
        return self.es.enter_context(self.nc.psum_tensor(name, list(shape), dt))

    def dma(self, q, out, in_, reads, writes, sem, **kw):
        self.s.op(q, lambda e: e.dma_start(out=out, in_=in_, **kw), reads, writes, dsem=sem)

    def mm(self, out, lhsT, rhs, start, stop, reads, writes):
        self.s.op("pe", lambda e: e.matmul(out, lhsT, rhs, start=start, stop=stop), reads, writes)

    def tr(self, out, in_, reads, writes):
        ident = self.ident
        self.s.op("pe", lambda e: e.transpose(out, in_, ident[:]), reads, writes)

    def act(self, out, in_, func, reads, writes, bias=None, scale=None, accum_out=None):
        kw = {}
        if bias is not None:
            kw["bias"] = bias
        if scale is not None:
            kw["scale"] = scale
        if accum_out is not None:
            kw["accum_out"] = accum_out
        self.s.op("act", lambda e: e.activation(out, in_, func, **kw), reads, writes)

    def tt(self, eng, out, in0, in1, op, reads, writes):
        self.s.op(eng, lambda e: e.tensor_tensor(out, in0, in1, op), reads, writes)

    def ts(self, eng, out, in0, s1, s2, op0, op1, reads, writes):
        if op1 is None:
            self.s.op(eng, lambda e: e.tensor_scalar(out, in0, s1, None, op0), reads, writes)
        else:
            self.s.op(eng, lambda e: e.tensor_scalar(out, in0, s1, s2, op0, op1), reads, writes)

    def stt(self, out, in0, scalar, in1, op0, op1, reads, writes):
        self.s.op("dve", lambda e: e.scalar_tensor_tensor(out, in0, scalar, in1, op0, op1), reads, writes)

    def newstat(self):
        c = self.ring("stat", 128)
        return self.stat[:, c:c + 1], ("stat", c)

    def build(self):
        nc = bass.Bass("TRN2", target_bir_lowering=False)
        self.nc = nc
        TP, TS = self.TP, self.TS
        self.x_p = self.dram_in("x_p", [TP, D])
        self.x_s = self.dram_in("x_s", [TS, D])
        self.st_hg = self.dram_in("st_hg", [self.nseq_s, 8, 128, 128])
        self.st_gl = self.dram_in("st_gl", [self.nseq_s, 4, 128, 256])
        self.w_in = self.dram_in("w_in", [D, NIN])
        self.w_out = self.dram_in("w_out", [D, D])
        self.w_gu = self.dram_in("w_gu", [D, 2 * DFF])
        self.w_dn = self.dram_in("w_dn", [DFF, D])
        self.w_gk2 = self.dram_in("w_gk2", [16, 512])
        self.c_lb = self.dram_in("c_lb", [128, 16])
        self.c_nmix = self.dram_in("c_nmix", [128, 16])
        self.c_nffn = self.dram_in("c_nffn", [128, 16])
        self.c_nfin = self.dram_in("c_nfin", [128, D])
        self.c_bgk = self.dram_in("c_bgk", [128, 4])
        self.c_hnw = self.dram_in("c_hnw", [128, 2, 512])
        self.c_mask = self.dram_in("c_mask", [128, 128])
        self.c_ident = self.dram_in("c_ident", [128, 128], BF16)
        self.c_sel = self.dram_in("c_sel", [128, 4])
        self.y_p = self.dram_out("y_p", [TP, D])
        self.y_s = self.dram_out("y_s", [TS, D])
        self.o_hg_p = self.dram_out("o_hg_p", [8, 128, 128])
        self.o_gl_p = self.dram_out("o_gl_p", [4, 128, 256])
        self.o_hg_s = self.dram_out("o_hg_s", [self.nseq_s, 8, 128, 128])
        self.o_gl_s = self.dram_out("o_gl_s", [self.nseq_s, 4, 128, 256])
        self.ws_in = nc.dram_tensor("ws_in", [D, NIN], BF16).ap()
        self.ws_out = nc.dram_tensor("ws_out", [D, D], BF16).ap()
        self.ws_gu = nc.dram_tensor("ws_gu", [D, 2 * DFF], BF16).ap()
        self.ws_dn = nc.dram_tensor("ws_dn", [DFF, D], BF16).ap()
        self.ag_src = nc.dram_tensor("ag_src", [128, 2048], F32).ap()
        self.ag_dst = nc.dram_tensor("ag_dst", [512, 2048], F32).ap()
        self.ag_src2 = nc.dram_tensor("ag_src2", [128, 64], F32).ap()
        self.ag_dst2 = nc.dram_tensor("ag_dst2", [512, 64], F32).ap()
        self.ag_src3 = nc.dram_tensor("ag_src3", [128, 64], F32).ap()
        self.ag_dst3 = nc.dram_tensor("ag_dst3", [512, 64], F32).ap()

        with contextlib.ExitStack() as es:
            self.es = es
            self.xres = self.sb("xres", [128, 4, D], F32)
            self.NST = 1
            self.stage = [self.sb(f"stage{i}", [128, D], F32) for i in range(self.NST)]
            self.xnb = [self.sb(f"xnb{i}", [128, D], BF16) for i in range(2)]
            self.hT = self.sb("hT", [128, KC, 512], BF16)
            self.ogT = self.sb("ogT", [128, KC, 512], BF16)
            self.NW = 3
            self.wt = [self.sb(f"wt{i}", [128, KC, 512], BF16) for i in range(self.NW)]
            self.vtok = self.sb("vtok", [128, 4, 512], BF16)
            self.sgtok = self.sb("sgtok", [128, 4, 512], BF16)
            self.qs = self.sb("qs", [128, 512], F32)
            self.fb = [self.sb(f"fb{i}", [128, 512], F32) for i in range(2)]
            self.kb = [self.sb(f"kb{i}", [128, 512], F32) for i in range(2)]
            self.cb = self.sb("cb", [128, 512], F32)
            self.e2 = self.sb("e2", [128, 512], F32)
            self.qt = [self.sb(f"qt{i}", [128, 512], BF16) for i in range(4)]
            self.kt = [self.sb(f"kt{i}", [128, 512], BF16) for i in range(4)]
            self.KT0 = self.sb("KT0", [128, 4, 128], BF16)
            self.KT1 = self.sb("KT1", [128, 4, 128], BF16)
            self.PTs = [self.sb(f"PTs{i}", [128, 4, 128], BF16) for i in range(3)]
            self.hstat = self.sb("hstat", [128, 32], F32)
            self.spbf = [self.sb(f"spbf{i}", [128, 256], BF16) for i in range(8)]
            self.ogbuf = self.sb("ogbuf", [128, 4, 512], BF16)
            self.junk = [self.sb(f"junk{i}", [128, 256], BF16) for i in range(2)]
            self.NACT = 8
            self.actT = self.sb("actT", [128, self.NACT, 512], BF16)
            self.Sbuf = self.sb("Sbuf", [128, D], F32)
            self.Sr = [self.sb(f"Sr{i}", [128, 256], F32) for i in range(2)]
            self.grT = self.sb("grT", [16, 512], F32)
            self.wgk2 = self.sb("wgk2", [16, 512], F32)
            self.nfin = self.sb("nfin", [128, D], F32)
            self.hnw = self.sb("hnw", [128, 2, 512], BF16)
            self.mask = self.sb("mask", [128, 128], F32)
            self.ident = self.sb("ident", [128, 128], BF16)
            self.stat = self.sb("stat", [128, 128], F32)
            self.cst = self.sb("cst", [128, 96], F32)
            self.tot = self.sb("tot", [128, 64], F32)
            self.dl = self.sb("dl", [128, 8], F32)
            self.Ach = self.sb("Ach", [128, 32], F32)
            self.bank = [self.ps(f"bank{i}", [128, 512], F32) for i in range(8)]
            self.esem = {e: es.enter_context(nc.semaphore(f"s_{e}")) for e in ["pe", "act", "dve", "pool"]}
            self.dsems = {}
            self.emit_program()
            self.s.finalize()
            for o in self.s.ops:
                if o.dsem is not None and o.dsem not in self.dsems:
                    self.dsems[o.dsem] = es.enter_context(nc.semaphore(f"d_{o.dsem}"))
            block = es.enter_context(nc.Block())
            self.emit_engines(block)
        return nc

    def emit_engines(self, block):
        per = {"pe": [], "act": [], "dve": [], "pool": [], "sp": []}
        for o in self.s.ops:
            per[o.eng].append(o)
        esem, dsems = self.esem, self.dsems

        def run(e, ops):
            for o in ops:
                for key, val in o.waits:
                    sem = dsems[key[1]] if key[0] == "d" else esem[key[1]]
                    e.wait_ge(sem, val)
                if o.fn is None:
                    continue
                ins = o.fn(e)
                if o.dsem is not None:
                    if o.dinc == 1:
                        ins.then_inc(dsems[o.dsem])
                    else:
                        ins.then_inc(dsems[o.dsem], o.dinc)
                elif o.marked:
                    ins.then_inc(esem[o.eng], 1)

        @block.tensor
        def _(e):
            run(e, per["pe"])

        @block.scalar
        def _(e):
            run(e, per["act"])

        @block.vector
        def _(e):
            run(e, per["dve"])

        @block.gpsimd
        def _(e):
            run(e, per["pool"])

        @block.sync
        def _(e):
            run(e, per["sp"])

    def emit_program(self):
        import os
        stop = os.environ.get("KSTOP", "all")
        self.emit_setup()
        self.zero_state()
        if stop != "setup" and self.nblk_p > 0:
            for b in range(self.nblk_p):
                self.emit_block(1, "p", b)
            if stop != "p1":
                if stop == "noxchg":
                    self.trickle_casts(len(self.cast_q))
                    self.zero_state()
                else:
                    self.emit_exchange()
                if stop != "xchg":
                    self.emit_stageA("p", 0)
                    for b in range(self.nblk_p):
                        if b + 1 < self.nblk_p:
                            hk = lambda b=b: self.emit_stageA("p", b + 1)
                        else:
                            hk = lambda: self.emit_stageA("s", 0)
                        self.emit_block(2, "p", b, do_A=False, hook=hk)
                    self.emit_prompt_state_out()
                    self.emit_block(2, "s", 0, do_A=False)
                    stop = "done"
        if stop in ("all", "noxchg"):
            self.emit_block(2, "s", 0)
        self.s.op("pool", None, reads=list(self.outkeys), writes=())

    def emit_setup(self):
        s = self.s
        self.outkeys = set()
        self.cast_q = []
        for name, src, dst, rows in (("in", self.w_in, self.ws_in, D), ("out", self.w_out, self.ws_out, D),
                                     ("gu", self.w_gu, self.ws_gu, D), ("dn", self.w_dn, self.ws_dn, DFF)):
            for r in range(rows // 128):
                for sub in range(4):
                    r0 = r * 128 + sub * 32
                    item = (dst[r0:r0 + 32, :], src[r0:r0 + 32, :], ("ws", name, r, sub), f"cast_{name}")
                    if name == "in":
                        self.emit_cast(item)
                    else:
                        self.cast_q.append(item)
        c = self.cst
        ld = lambda out, in_, key: self.dma("sp", out, in_, [], [key], "const")
        ld(c[:, 0:16], self.c_lb, ("cst", "lbl"))
        ld(c[:, 32:48], self.c_nmix, ("cst", "nmix"))
        ld(c[:, 48:64], self.c_nffn, ("cst", "nffn"))
        ld(c[:, 64:68], self.c_bgk, ("cst", "bgk"))
        ld(c[:, 72:76], self.c_sel, ("cst", "sel"))
        ld(self.nfin[:], self.c_nfin, ("nfin",))
        self.dma("pool", self.hnw[:], self.c_hnw, [], [("hnw",)], "const2")
        ld(self.mask[:], self.c_mask, ("mask",))
        ld(self.ident[:], self.c_ident, ("ident",))
        ld(self.wgk2[:], self.w_gk2, ("wgk2",))
        s.op("dve", lambda e: e.memset(c[:, 68:69], EPS), [], [("cst", "eps")])
        s.op("dve", lambda e: e.memset(c[:, 69:70], 1.0), [], [("cst", "one")])
        s.op("dve", lambda e: e.memset(c[:, 70:71], LNCQ), [], [("cst", "lncq")])
        s.op("dve", lambda e: e.memset(self.tot[:], 0.0), [], [("tot",)])
        s.op("pool", lambda e: e.memset(self.KT0[:], 0.0), [], [("KT",)])
        s.op("pool", lambda e: e.memset(self.KT1[:], 0.0), [], [("KT",)])
        self.tt("dve", c[:, 16:24], c[:, 0:8], c[:, 8:16], ALU.subtract, [("cst", "lbl")], [("cst", "lb")])
        self.act(c[:, 16:24], c[:, 16:24], AF.Sigmoid, [("cst", "lb")], [("cst", "lb")])
        self.ts("dve", c[:, 24:32], c[:, 16:24], -1.0, 1.0, ALU.mult, ALU.add, [("cst", "lb")], [("cst", "oml")])
        self.ts("dve", c[:, 64:68], c[:, 64:68], -1.0, None, ALU.mult, None, [("cst", "bgk")], [("cst", "bgk")])
        self.const_reads = [("cst", k) for k in ("lb", "oml", "nmix", "nffn", "bgk", "eps", "one", "lncq", "sel")]

    def emit_cast(self, item):
        dst, src, key, sem = item
        self.dma("pool", dst, src, [], [key], sem, max_dma_last_dim=4096)

    def trickle_casts(self, n):
        for _ in range(n):
            if self.cast_q:
                self.emit_cast(self.cast_q.pop(0))

    def zero_state(self):
        self.s.op("dve", lambda e: e.memset(self.Sbuf[:], 0.0), [], [("S", h) for h in range(12)])

    def wload(self, which, r0, nr, c0, ncol):
        slot = self.ring("w", self.NW)
        src = {"in": self.ws_in, "out": self.ws_out, "gu": self.ws_gu, "dn": self.ws_dn}[which]
        t = self.wt[slot]
        self.dma("sp", t[:, 0:nr, 0:ncol],
                 src[r0:r0 + nr * 128, c0:c0 + ncol].rearrange("(k p) n -> p k n", p=128),
                 [("ws", which, r_, sub) for r_ in range(r0 // 128, r0 // 128 + nr) for sub in range(4)],
                 [("w", slot)], f"w{slot}")
        return t, ("w", slot)

    def next_acc(self):
        a = self.ring("bank", 7)
        return self.bank[a], ("bank", a)

    def dummies(self, n):
        for _ in range(n):
            self.s.op("pe", lambda e: e.matmul(self.bank[7][:, 0:512], self.ident[:], self.hnw[:, 0, :],
                                               start=True, stop=True), [("ident",), ("hnw",)], [])

    def trm(self, out, in_, reads, writes):
        ident = self.ident
        self.s.op("pe", lambda e: e.matmul(out, in_, ident[:], start=True, stop=True), reads + [("ident",)], writes)

    def norm_transpose(self, src_ap, src_keys, nw_lo, dstT, dst_key, i):
        j = self.ring("xnb", 2)
        xn = self.xnb[j]
        ss, ssk = self.newstat()
        self.act(xn[:], src_ap, AF.Square, src_keys, [("xnb", j), ssk], accum_out=ss)
        rt, rtk = self.newstat()
        self.act(rt, ss, AF.Ln, [ssk, ("cst", "eps")], [rtk], bias=self.cst[:, 68:69], scale=1.0 / D)
        rs, rsk = self.newstat()
        self.act(rs, rt, AF.Exp, [rtk], [rsk], scale=-0.5)
        self.ts("dve", xn[:], src_ap, rs, None, ALU.mult, None, src_keys + [rsk], [("xnb", j)])
        for q in range(4):
            bk, bkk = self.next_acc()
            for r in range(4):
                kc = q * 4 + r
                self.trm(bk[:, r * 128:(r + 1) * 128], xn[:, kc * 128:(kc + 1) * 128], [("xnb", j)], [bkk])
            nwb = self.cst[:, nw_lo + q * 4: nw_lo + q * 4 + 4].unsqueeze(2).to_broadcast([128, 4, 128])
            self.tt("dve", dstT[:, q * 4:(q + 1) * 4, i * 128:(i + 1) * 128],
                    bk[:].rearrange("p (a b) -> p a b", a=4), nwb, ALU.mult,
                    [bkk, ("cst", "nmix"), ("cst", "nffn")], [dst_key(i)])
        return rs, rsk

    def emit_stageA(self, kind, b):
        T = 512 if kind == "p" else self.TS
        NT = T // 128
        xsrc = self.x_p if kind == "p" else self.x_s
        t0 = b * 512
        hTk = lambda i: ("hT", i)
        for i in range(NT):
            sidx = self.ring("stage", self.NST)
            st = self.stage[sidx]
            self.dma("sp", st[:], xsrc[t0 + i * 128: t0 + (i + 1) * 128, :], [], [("stage", sidx)], f"st{sidx}")
            self.norm_transpose(st[:], [("stage", sidx)], 32, self.hT, hTk, i)

    def emit_block(self, ph, kind, b, do_A=True, hook=None):
        T = 512 if kind == "p" else self.TS
        NT = T // 128
        xsrc = self.x_p if kind == "p" else self.x_s
        ydst = self.y_p if kind == "p" else self.y_s
        t0 = b * 512
        if do_A:
            self.emit_stageA(kind, b)
        if ph == 2:
            for i in range(NT):
                self.dma("sp", self.xres[:, i, :], xsrc[t0 + i * 128: t0 + (i + 1) * 128, :], [],
                         [("xres", i, g) for g in range(4)], f"xr{i}")
        hT_all = [("hT", i) for i in range(NT)]
        wtg, wk = self.wload("in", 0, KC, O_GR, 16)
        a, ak = self.next_acc()
        for kc in range(KC):
            self.mm(a[0:16, 0:T], wtg[:, kc, 0:16], self.hT[:, kc, 0:T], kc == 0, kc == KC - 1, [wk] + hT_all, [ak])
        self.s.op("dve", lambda e, a=a, T=T: e.tensor_copy(self.grT[:, 0:T], a[0:16, 0:T]), [ak], [("grT",)])
        groups = [
            dict(kind="hg", heads=[0, 1, 2, 3], vcol=O_HI, gcol=O_HGATE, qcol=O_HQ, kcol=O_HF, dv=128, hw=0, oc=0),
            dict(kind="hg", heads=[4, 5, 6, 7], vcol=O_HI + 512, gcol=O_HGATE + 512, qcol=O_HQ + 512, kcol=O_HF + 512, dv=128, hw=0, oc=512),
            dict(kind="gl", heads=[0, 1], vcol=O_GV, gcol=O_GGATE, qcol=O_GQ, kcol=O_GK, dv=256, hw=1, oc=1024),
            dict(kind="gl", heads=[2, 3], vcol=O_GV + 512, gcol=O_GGATE + 512, qcol=O_GQ + 256, kcol=O_GK + 256, dv=256, hw=1, oc=1536),
        ]
        for gi, g in enumerate(groups):
            self.emit_group(ph, kind, b, T, NT, g, gi, hT_all)
        if ph == 1:
            return
        import os
        dbg = os.environ.get("KDBG", "")
        if "noE" in dbg:
            return
        ogT_all = [("ogT", i) for i in range(NT)]
        for cg in range(4):
            wt, wk = self.wload("out", 0, KC, cg * 512, 512)
            for i in range(NT):
                a, ak = self.next_acc()
                for kc in range(KC):
                    self.mm(a[:], self.ogT[:, kc, i * 128:(i + 1) * 128], wt[:, kc, :], kc == 0, kc == KC - 1,
                            [wk, ("ogT", i)], [ak])
                xs = self.xres[:, i, cg * 512:(cg + 1) * 512]
                self.tt("dve", xs, a[:], xs, ALU.add, [ak, ("xres", i, cg)], [("xres", i, cg)])
        if "noF" in dbg:
            return
        for i in range(NT):
            self.norm_transpose(self.xres[:, i, :], [("xres", i, g_) for g_ in range(4)], 48, self.ogT,
                                lambda i_: ("ogT", i_), i)
        h2_all = [("ogT", i) for i in range(NT)]
        halves = [(0, 8), (8, 8), (16, 8), (24, 8), (32, 8), (40, 4)]
        for hidx_, (c0, nchk) in enumerate(halves):
            if hidx_ == 1 and hook is not None:
                hook()
            for fg in range(nchk // 4):
                cc0 = c0 + fg * 4
                wg, wgk = self.wload("gu", 0, KC, cc0 * 128, 512)
                wu, wuk = self.wload("gu", 0, KC, DFF + cc0 * 128, 512)
                for cb in range(4):
                    ag, agk = self.next_acc()
                    for kc in range(KC):
                        self.mm(ag[:, 0:T], wg[:, kc, cb * 128:(cb + 1) * 128], self.ogT[:, kc, 0:T], kc == 0,
                                kc == KC - 1, [wgk] + h2_all, [agk])
                    au, auk = self.next_acc()
                    for kc in range(KC):
                        self.mm(au[:, 0:T], wu[:, kc, cb * 128:(cb + 1) * 128], self.ogT[:, kc, 0:T], kc == 0,
                                kc == KC - 1, [wuk] + h2_all, [auk])
                    self.act(self.e2[:, 0:T], ag[:, 0:T], AF.Silu, [agk], [("e2",)])
                    lc = fg * 4 + cb
                    self.tt("dve", self.actT[:, lc, 0:T], self.e2[:, 0:T], au[:, 0:T], ALU.mult,
                            [("e2",), auk], [("actT", lc)])
            npieces = 1
            pk = nchk // npieces
            for cg in range(4):
                accs = [self.next_acc() for _ in range(NT)]
                for pc in range(npieces):
                    wd, wdk = self.wload("dn", (c0 + pc * pk) * 128, pk, cg * 512, 512)
                    for kk in range(pk):
                        lc = pc * pk + kk
                        for i in range(NT):
                            a, ak = accs[i]
                            self.mm(a[:], self.actT[:, lc, i * 128:(i + 1) * 128], wd[:, kk, :], lc == 0,
                                    lc == nchk - 1, [wdk, ("actT", lc)], [ak])
                for i in range(NT):
                    a, ak = accs[i]
                    xs = self.xres[:, i, cg * 512:(cg + 1) * 512]
                    self.tt("dve", xs, a[:], xs, ALU.add, [ak, ("xres", i, cg)], [("xres", i, cg)])
        if "noI" in dbg:
            return
        for i in range(NT):
            xk = [("xres", i, g_) for g_ in range(4)]
            j = self.ring("xnb", 2)
            ss, ssk = self.newstat()
            self.act(self.xnb[j][:], self.xres[:, i, :], AF.Square, xk, [("xnb", j), ssk], accum_out=ss)
            rt, rtk = self.newstat()
            self.act(rt, ss, AF.Ln, [ssk, ("cst", "eps")], [rtk], bias=self.cst[:, 68:69], scale=1.0 / D)
            rs, rsk = self.newstat()
            self.act(rs, rt, AF.Exp, [rtk], [rsk], scale=-0.5)
            self.stt(self.xres[:, i, :], self.xres[:, i, :], rs, self.nfin[:], ALU.mult, ALU.mult,
                     xk + [rsk, ("nfin",)], xk)
            ok = ("y", kind, b, i)
            self.dma("pool", ydst[t0 + i * 128: t0 + (i + 1) * 128, :], self.xres[:, i, :], xk, [ok] + xk, "yout")
            self.outkeys.add(ok)

    def emit_group(self, ph, kind, b, T, NT, g, gi, hT_all):
        dv = g["dv"]
        is_hg = g["kind"] == "hg"
        gs = 1.0 if is_hg else -1.0 / 16.0
        nch = T // CH
        heads = g["heads"]
        nh = len(heads)
        c = self.cst
        wv, wvk = self.wload("in", 0, KC, g["vcol"], 512)
        for i in range(NT):
            a, ak = self.next_acc()
            for kc in range(KC):
                self.mm(a[:], self.hT[:, kc, i * 128:(i + 1) * 128], wv[:, kc, :], kc == 0, kc == KC - 1,
                        [wvk, ("hT", i)], [ak])
            self.act(self.vtok[:, i, :], a[:], AF.Copy, [ak], [("vtok", i)])
        if ph == 2:
            wg, wgk = self.wload("in", 0, KC, g["gcol"], 512)
            for i in range(NT):
                a, ak = self.next_acc()
                for kc in range(KC):
                    self.mm(a[:], self.hT[:, kc, i * 128:(i + 1) * 128], wg[:, kc, :], kc == 0, kc == KC - 1,
                            [wgk, ("hT", i)], [ak])
                self.act(self.sgtok[:, i, :], a[:], AF.Silu, [ak], [("sgtok", i)])
                self.tt("pool", self.sgtok[:, i, :], self.sgtok[:, i, :], self.hnw[:, g["hw"], :], ALU.mult,
                        [("sgtok", i), ("hnw",)], [("sgtok", i)])
        ncolqk = 128 * nh
        if ph == 2:
            wq, wqk = self.wload("in", 0, KC, g["qcol"], ncolqk)
        wk_, wkk = self.wload("in", 0, KC, g["kcol"], ncolqk)
        def early(hi_, h):
            p = hi_ % 2
            fb, kb = self.fb[p], self.kb[p]
            fbk, kbk = ("fb", p), ("kb", p)
            a, ak = self.next_acc()
            for kc in range(KC):
                self.mm(a[:, 0:T], wk_[:, kc, hi_ * 128:(hi_ + 1) * 128], self.hT[:, kc, 0:T], kc == 0, kc == KC - 1,
                        [wkk] + hT_all, [ak])
            aqh = None
            if ph == 2:
                aq, aqk = self.next_acc()
                for kc in range(KC):
                    self.mm(aq[:, 0:T], wq[:, kc, hi_ * 128:(hi_ + 1) * 128], self.hT[:, kc, 0:T], kc == 0,
                            kc == KC - 1, [wqk] + hT_all, [aqk])
                aqh = (aq, aqk)
            if is_hg:
                self.act(fb[:, 0:T], a[:, 0:T], AF.Sigmoid, [ak], [fbk])
                self.ts("dve", fb[:, 0:T], fb[:, 0:T], c[:, 24 + h:25 + h], c[:, 16 + h:17 + h], ALU.mult,
                        ALU.add, [fbk, ("cst", "lb"), ("cst", "oml")], [fbk])
                self.ts("dve", kb[:, 0:T], fb[:, 0:T], -1.0, 1.0, ALU.mult, ALU.add, [fbk], [kbk])
                self.act(fb[:, 0:T], fb[:, 0:T], AF.Ln, [fbk], [fbk])
            else:
                self.act(kb[:, 0:T], a[:, 0:T], AF.Copy, [ak], [kbk])
                a2, a2k = self.next_acc()
                self.mm(a2[:, 0:T], self.wgk2[:, h * 128:(h + 1) * 128], self.grT[:, 0:T], True, True,
                        [("wgk2",), ("grT",)], [a2k])
                self.act(fb[:, 0:T], a2[:, 0:T], AF.Exp, [a2k, ("cst", "bgk")], [fbk],
                         bias=c[:, 64 + h:65 + h], scale=-1.0)
                self.act(fb[:, 0:T], fb[:, 0:T], AF.Ln, [fbk, ("cst", "one")], [fbk],
                         bias=c[:, 69:70], scale=1.0)
            return aqh

        def late(hi_, h, aqh):
            p = hi_ % 2
            fb, kb = self.fb[p], self.kb[p]
            fbk, kbk = ("fb", p), ("kb", p)
            hidx = h if is_hg else 8 + h
            qt, kt = self.qt[hi_], self.kt[hi_]
            Ach = self.Ach[:, hi_ * 8: hi_ * 8 + 8]
            Achk = ("Ach", hi_)
            self.s.op("dve", lambda e, T=T, fb=fb: e.tensor_tensor_scan(self.cb[:, 0:T], self.cst[:, 69:70].to_broadcast([128, T]),
                                                                         fb[:, 0:T], 0.0, ALU.mult, ALU.add),
                      [fbk, ("cst", "one")], [("cb",)])
            cb3 = self.cb[:, 0:T].rearrange("p (c t) -> p c t", t=CH)
            fb3 = fb[:, 0:T].rearrange("p (c t) -> p c t", t=CH)
            lastv = cb3[:, :, CH - 1]
            self.tt("dve", fb3, cb3, cb3[:, :, CH - 1:CH].to_broadcast([128, nch, CH]), ALU.subtract,
                    [("cb",)], [fbk])
            dl = self.dl
            self.s.op("dve", lambda e, lastv=lastv, dl=dl: e.tensor_copy(dl[:, 0:1], lastv[:, 0:1]), [("cb",)], [("dl",)])
            if nch > 1:
                self.tt("dve", dl[:, 1:nch], lastv[:, 1:nch], lastv[:, 0:nch - 1], ALU.subtract, [("cb",)], [("dl",)])
            self.act(Ach[:, 0:nch], dl[:, 0:nch], AF.Exp, [("dl",)], [Achk], scale=gs)
            if ph == 1:
                self.tt("dve", self.tot[:, hidx:hidx + 1], self.tot[:, hidx:hidx + 1], lastv[:, nch - 1:nch], ALU.add,
                        [("cb",), ("tot",)], [("tot",)])
            if ph == 2:
                self.act(self.e2[:, 0:T], fb[:, 0:T], AF.Exp, [fbk, ("cst", "lncq")], [("e2",)],
                         bias=c[:, 70:71], scale=gs)
            self.act(fb[:, 0:T], fb[:, 0:T], AF.Exp, [fbk], [fbk], scale=-gs)
            self.tt("pool", kt[:, 0:T], kb[:, 0:T], fb[:, 0:T], ALU.mult, [kbk, fbk], [("kt", hi_)])
            if ph == 1:
                self.trickle_casts(4)
            if ph == 2:
                aq, aqk = aqh
                self.act(self.qs[:, 0:T], aq[:, 0:T], AF.Silu if is_hg else AF.Copy, [aqk], [("qs",)])
                self.tt("pool", qt[:, 0:T], self.qs[:, 0:T], self.e2[:, 0:T], ALU.mult, [("qs",), ("e2",)], [("qt", hi_)])

        pend = None
        for hi_, h in enumerate(heads):
            aqh = early(hi_, h)
            if pend is not None:
                late(*pend)
            pend = (hi_, h, aqh)
        late(*pend)
        hpb = 512 // (2 * dv)

        def stage_P(i):
            tsl = slice(i * 128, (i + 1) * 128)
            bx, bxk = self.next_acc()
            for hi_ in range(nh):
                self.trm(bx[:, hi_ * 128:(hi_ + 1) * 128], self.kt[hi_][:, tsl], [("kt", hi_)], [bxk])
            if ph == 2:
                bs, bsk = self.next_acc()
                for hi_ in range(nh):
                    self.mm(bs[:, hi_ * 128:(hi_ + 1) * 128], self.kt[hi_][:, tsl], self.qt[hi_][:, tsl], True, True,
                            [("kt", hi_), ("qt", hi_)], [bsk])
            w = nh * 128
            self.act(self.KT0[0:64, 0:nh, :], bx[0:64, 0:w].rearrange("p (a b) -> p a b", a=nh), AF.Copy, [bxk], [("KT",)])
            self.act(self.KT1[64:128, 0:nh, :], bx[64:128, 0:w].rearrange("p (a b) -> p a b", a=nh), AF.Copy, [bxk],
                     [("KT",)])
            pr = None
            if ph == 2:
                pr = self.ring("PTs", 3)
                self.tt("dve", self.PTs[pr][:, 0:nh, :], bs[:, 0:w].rearrange("p (a b) -> p a b", a=nh),
                        self.mask[:].unsqueeze(1).to_broadcast([128, nh, 128]), ALU.mult, [bsk, ("mask",)], [("PTs", pr)])
            Us = {}
            bu = buk = None
            self.dummies(NDUM)
            for hi_ in range(nh):
                vc = hi_ * dv
                if hi_ % hpb == 0:
                    bu, buk = self.next_acc()
                off = (hi_ % hpb) * 2 * dv
                for cc in range(2):
                    KTc = self.KT0 if cc == 0 else self.KT1
                    self.mm(bu[:, off + cc * dv: off + (cc + 1) * dv], KTc[:, hi_, :], self.vtok[:, i, vc:vc + dv],
                            True, True, [("KT",), ("vtok", i)], [buk])
                    Us[(hi_, cc)] = (bu[:, off + cc * dv: off + (cc + 1) * dv], buk)
            return (i, Us, pr)

        def stage_Q(i, Us, pr):
            spss = {}
            for hi_, h in enumerate(heads):
                hidx = h if is_hg else 8 + h
                scol = h * 128 if is_hg else 1024 + h * 256
                Achk = ("Ach", hi_)
                sps = []
                for cc in range(2):
                    cidx = i * 2 + cc
                    U, buk = Us[(hi_, cc)]
                    if kind == "p":
                        S_ap = self.Sbuf[:, scol:scol + dv]
                        Sk = ("S", hidx)
                    else:
                        sr = self.ring("Sr", 2)
                        S_ap = self.Sr[sr][:, 0:dv]
                        Sk = ("Sr", sr)
                        seq = cidx
                        src = self.st_hg[seq, h] if is_hg else self.st_gl[seq, h]
                        self.dma("sp", S_ap, src, [], [Sk], f"sr{sr}")
                    Acol = self.Ach[:, hi_ * 8 + cidx: hi_ * 8 + cidx + 1]
                    if ph == 2:
                        spr = self.ring("spbf", 8)
                        sp_ = self.spbf[spr][:, 0:dv]
                        self.act(sp_, S_ap, AF.Copy, [Sk, Achk], [("spbf", spr)], scale=Acol)
                        sps.append((sp_, ("spbf", spr)))
                    self.stt(S_ap, S_ap, Acol, U, ALU.mult, ALU.add, [Sk, Achk, buk], [Sk])
                    if kind == "s" and ph == 2:
                        dst = self.o_hg_s[seq, h] if is_hg else self.o_gl_s[seq, h]
                        ok = ("so", seq, hidx)
                        self.dma("pool", dst, S_ap, [Sk], [ok, Sk], "sout")
                        self.outkeys.add(ok)
                spss[hi_] = sps
            if ph == 2:
                self.dummies(NDUM)
                bo, bok = self.next_acc()
                for hi_ in range(nh):
                    vc = hi_ * dv
                    qt = self.qt[hi_]
                    self.mm(bo[:, vc:vc + dv], self.PTs[pr][:, hi_, :], self.vtok[:, i, vc:vc + dv], True, False,
                            [("PTs", pr), ("vtok", i)], [bok])
                    for cc in range(2):
                        sp_, spk = spss[hi_][cc]
                        p0 = cc * 64
                        self.mm(bo[p0:p0 + 64, vc:vc + dv], qt[:, i * 128 + p0: i * 128 + p0 + 64], sp_, False, True,
                                [("qt", hi_), spk], [bok])
                c0 = self.ring("hstat", 8) * 4
                hsb = self.hstat[:, c0:c0 + nh]
                hskeys = [("hstat", c0 + j) for j in range(nh)]
                for hi_ in range(nh):
                    vc = hi_ * dv
                    jk = self.ring("junk", 2)
                    self.act(self.junk[jk][:, 0:dv], bo[:, vc:vc + dv], AF.Square, [bok], [("junk", jk), hskeys[hi_]],
                             accum_out=self.hstat[:, c0 + hi_:c0 + hi_ + 1])
                self.act(hsb, hsb, AF.Ln, hskeys + [("cst", "eps")], hskeys, bias=c[:, 68:69], scale=1.0 / dv)
                self.act(hsb, hsb, AF.Exp, hskeys, hskeys, scale=-0.5)
                for hi_ in range(nh):
                    vc = hi_ * dv
                    self.stt(self.ogbuf[:, i, vc:vc + dv], bo[:, vc:vc + dv], self.hstat[:, c0 + hi_:c0 + hi_ + 1],
                             self.sgtok[:, i, vc:vc + dv], ALU.mult, ALU.mult,
                             [bok, hskeys[hi_], ("sgtok", i)], [("ogbuf", i)])

        pendP = None
        for i in range(NT):
            cur = stage_P(i)
            if pendP is not None:
                stage_Q(*pendP)
            pendP = cur
        stage_Q(*pendP)
        if ph == 2:
            for i in range(NT):
                bk, bkk = self.next_acc()
                for rr in range(4):
                    self.trm(bk[:, rr * 128:(rr + 1) * 128], self.ogbuf[:, i, rr * 128:(rr + 1) * 128],
                             [("ogbuf", i)], [bkk])
                q = g["oc"] // 512
                self.act(self.ogT[:, q * 4:(q + 1) * 4, i * 128:(i + 1) * 128],
                         bk[:].rearrange("p (a b) -> p a b", a=4), AF.Copy, [bkk], [("ogT", i)])

    def emit_exchange(self):
        s = self.s
        self.trickle_casts(len(self.cast_q))
        Skeys = [("S", h) for h in range(12)]
        self.dma("pool", self.ag_src[:, :], self.Sbuf[:], Skeys, [("agsrc",)], "ag1")
        self.dma("pool", self.ag_src2[:, :], self.tot[:], [("tot",)], [("agsrc2",)], "ag1")
        s.op("pool", lambda e: e.collective_compute("AllGather", ALU.bypass, replica_groups=[[0, 1, 2, 3], [4, 5, 6, 7]],
                                                    ins=[self.ag_src], outs=[self.ag_dst]),
             [("agsrc",)], [("agdst",)], dsem="agcc", dinc=1)
        s.op("pool", lambda e: e.collective_compute("AllGather", ALU.bypass, replica_groups=[[0, 1, 2, 3], [4, 5, 6, 7]],
                                                    ins=[self.ag_src2], outs=[self.ag_dst2]),
             [("agsrc2",), ("agdst",)], [("agdst2",)], dsem="agcc2", dinc=1)
        self.dma("pool", self.ag_src3[:, :], self.tot[:], [("tot",)], [("agsrc3",)], "ag1")
        s.op("pool", lambda e: e.collective_compute("AllGather", ALU.bypass, replica_groups=[[0, 1, 2, 3], [4, 5, 6, 7]],
                                                    ins=[self.ag_src3], outs=[self.ag_dst3]),
             [("agsrc3",), ("agdst",), ("agdst2",)], [("agdst",), ("agdst2",), ("agdst3",)], dsem="agcc3", dinc=1)
        R = self.xres[:, 0, :]
        L = self.xres[:, 1, :]
        Rk = [("xres", 0, g) for g in range(4)]
        Lk = [("xres", 1, g) for g in range(4)]
        c = self.cst
        s.op("dve", lambda e: e.memset(self.Sbuf[:], 0.0), [], Skeys)
        self.dma("sp", R, self.ag_dst[0:128, 0:D], [("agdst",)], Rk, "agl")
        self.stt(self.Sbuf[:], R, c[:, 73:74], self.Sbuf[:], ALU.mult, ALU.add, Rk + Skeys + [("cst", "sel")], Skeys)
        for k in (1, 2):
            self.dma("sp", L, self.ag_dst[k * 128:(k + 1) * 128, 0:D], [("agdst",)], Lk, "agl")
            self.dma("sp", self.tot[:, 0:16], self.ag_dst2[k * 128:(k + 1) * 128, 0:16], [("agdst2",)], [("tot",)], "agl")
            self.act(c[:, 76:84], self.tot[:, 0:8], AF.Exp, [("tot",)], [("cst", "atot")], scale=1.0)
            self.act(c[:, 84:88], self.tot[:, 8:12], AF.Exp, [("tot",)], [("cst", "atot")], scale=-1.0 / 16.0)
            for h in range(12):
                sc, dv = (h * 128, 128) if h < 8 else (1024 + (h - 8) * 256, 256)
                self.stt(R[:, sc:sc + dv], R[:, sc:sc + dv], c[:, 76 + h:77 + h], L[:, sc:sc + dv], ALU.mult, ALU.add,
                         Rk + Lk + [("cst", "atot")], Rk)
            self.stt(self.Sbuf[:], R, c[:, 73 + k:74 + k], self.Sbuf[:], ALU.mult, ALU.add,
                     Rk + Skeys + [("cst", "sel")], Skeys)

    def emit_prompt_state_out(self):
        Skeys = [("S", h) for h in range(12)]
        ok = ("spo",)
        self.dma("pool", self.o_hg_p.rearrange("h k v -> k h v"),
                 self.Sbuf[:, 0:1024].rearrange("k (h v) -> k h v", h=8), Skeys, [ok], "sout")
        self.dma("pool", self.o_gl_p.rearrange("h k v -> k h v"),
                 self.Sbuf[:, 1024:2048].rearrange("k (h v) -> k h v", h=4), Skeys, [ok], "sout")
        self.outkeys.add(ok)


def _build(nblk_p=8, nseq_s=4):
    k = Kern(nblk_p, nseq_s)
    return k


def make_inputs(inputs, nblk_p=8, nseq_s=4, ncores=8):
    f32 = np.float32
    xp = np.asarray(inputs["x_prompt"], f32)
    xs = np.asarray(inputs["x_sample"], f32)
    shg = np.asarray(inputs["state_hgrn"], f32)[0]
    sgl = np.asarray(inputs["state_gla"], f32)[0]
    TP = nblk_p * 512
    lbl = np.asarray(inputs["lb_logits"], f32)
    c_lb = np.ascontiguousarray(lbl.reshape(2, 8, 128).transpose(2, 0, 1).reshape(128, 16))
    feat = lambda v: np.ascontiguousarray(np.asarray(v, f32).reshape(-1, 128).T)
    c_nmix = feat(inputs["norm_mix"][0])
    c_nffn = feat(inputs["norm_ffn"][0])
    c_nfin = np.ascontiguousarray(np.broadcast_to(np.asarray(inputs["norm_final"], f32)[None, :], (128, D)))
    c_bgk = feat(inputs["b_gk"][0])
    hgn = np.asarray(inputs["hg_norm"], f32)[0]
    gln = np.asarray(inputs["gla_norm"], f32)[0]
    c_hnw = np.ascontiguousarray(np.broadcast_to(
        np.stack([np.tile(hgn, 4), np.tile(gln, 2)])[None], (128, 2, 512)))
    idx = np.arange(128)
    mask = ((idx[:, None] // 64 == idx[None, :] // 64) & (idx[:, None] <= idx[None, :])).astype(f32)
    ident = np.eye(128, dtype=f32).astype(ml_dtypes.bfloat16)
    common = dict(
        w_in=np.ascontiguousarray(np.asarray(inputs["w_in"], f32)[0]),
        w_out=np.ascontiguousarray(np.asarray(inputs["w_out"], f32)[0]),
        w_gu=np.ascontiguousarray(np.asarray(inputs["w_gate_up"], f32)[0]),
        w_dn=np.ascontiguousarray(np.asarray(inputs["w_down"], f32)[0]),
        w_gk2=np.ascontiguousarray(np.asarray(inputs["w_gk2"], f32)[0]),
        c_lb=c_lb, c_nmix=c_nmix, c_nffn=c_nffn, c_nfin=c_nfin, c_bgk=c_bgk, c_hnw=c_hnw,
        c_mask=mask, c_ident=ident,
    )
    maps = []
    for c in range(ncores):
        bseq, seg = c // 4, c % 4
        sel = np.zeros((128, 4), f32)
        sel[:, seg] = 1.0
        m = dict(common)
        m["x_p"] = np.ascontiguousarray(xp[bseq, seg * TP:(seg + 1) * TP])
        m["x_s"] = np.ascontiguousarray(xs[c * nseq_s:(c + 1) * nseq_s].reshape(nseq_s * 64, D))
        m["st_hg"] = np.ascontiguousarray(shg[c * nseq_s:(c + 1) * nseq_s])
        m["st_gl"] = np.ascontiguousarray(sgl[c * nseq_s:(c + 1) * nseq_s])
        m["c_sel"] = sel
        maps.append(m)
    return maps


_NC_CACHE = {}


def run(inputs, nblk_p=8, nseq_s=4):
    key = (nblk_p, nseq_s)
    kern = Kern(nblk_p, nseq_s)
    nc = kern.build()
    maps = make_inputs(inputs, nblk_p, nseq_s)
    res = run_bass_kernel_spmd(nc, maps, core_ids=list(range(8)))
    R = res.results
    TP = nblk_p * 512
    y_p = np.stack([np.concatenate([R[b * 4 + s]["y_p"] for s in range(4)], axis=0) for b in range(2)])
    y_s = np.concatenate([R[c]["y_s"].reshape(nseq_s, 64, D) for c in range(8)], axis=0)
    hg_p = np.stack([R[3]["o_hg_p"], R[7]["o_hg_p"]])[None]
    gl_p = np.stack([R[3]["o_gl_p"], R[7]["o_gl_p"]])[None]
    hg_s = np.concatenate([R[c]["o_hg_s"] for c in range(8)], axis=0)[None]
    gl_s = np.concatenate([R[c]["o_gl_s"] for c in range(8)], axis=0)[None]
    f = lambda a: np.ascontiguousarray(a, dtype=np.float32)
    return (f(y_p), f(y_s), f(hg_p), f(gl_p), f(hg_s), f(gl_s))


def kernel(**inputs):
    return run(inputs, 8, 4)
```

```python
import contextlib
import os
DBG = os.environ.get('KDBG', '')
import math
import numpy as np
import ml_dtypes
import concourse.bass as bass
import concourse.mybir as mybir
from concourse.bass_utils import run_bass_kernel_spmd

F32 = mybir.dt.float32
BF16 = mybir.dt.bfloat16
AF = mybir.ActivationFunctionType
ALU = mybir.AluOpType

D = 2048
KC = 16
NIN = 7184
DFF = 5632
EPS = 1e-6
CH = 64
LNCQ = math.log(128.0 ** -0.5)

O_HQ, O_HF, O_HI, O_HGATE, O_GQ, O_GK, O_GV, O_GGATE, O_GR = 0, 1024, 2048, 3072, 4096, 4608, 5120, 6144, 7168


class Op:
    __slots__ = ("eng", "fn", "deps", "dsem", "idx", "waits", "marked", "count", "dcount", "dinc")


class Sched:
    def __init__(self):
        self.ops = []
        self.last_w = {}
        self.readers = {}

    def op(self, eng, fn, reads=(), writes=(), dsem=None, dinc=16):
        o = Op()
        o.eng, o.fn, o.dsem, o.idx = eng, fn, dsem, len(self.ops)
        o.dinc = dinc
        o.marked = False
        deps = set()
        for k in reads:
            w = self.last_w.get(k)
            if w is not None:
                deps.add(w)
        for k in writes:
            w = self.last_w.get(k)
            if w is not None:
                deps.add(w)
            r = self.readers.get(k)
            if r:
                deps.update(r)
        for k in reads:
            rl = self.readers.setdefault(k, [])
            if dsem is None:
                for j in range(len(rl)):
                    if rl[j].dsem is None and rl[j].eng == eng:
                        rl[j] = o
                        break
                else:
                    rl.append(o)
            else:
                rl.append(o)
        for k in writes:
            self.last_w[k] = o
            self.readers[k] = []
        deps.discard(o)
        o.deps = deps
        self.ops.append(o)
        return o

    def finalize(self):
        for o in self.ops:
            for d in o.deps:
                if d.dsem is None:
                    if d.eng == o.eng and d.eng == "pe" and o.dsem is None:
                        continue
                    d.marked = True
        cnt = {}
        dcnt = {}
        dma_hist = {}
        for o in self.ops:
            if o.dsem is not None:
                dcnt[o.dsem] = dcnt.get(o.dsem, 0) + o.dinc
                o.dcount = dcnt[o.dsem]
                dma_hist.setdefault(o.dsem, []).append((o.idx, o.dcount))
            elif o.marked:
                cnt[o.eng] = cnt.get(o.eng, 0) + 1
                o.count = cnt[o.eng]
        waited = {}
        import bisect
        for o in self.ops:
            need = {}
            for d in o.deps:
                if d.dsem is not None:
                    hist = dma_hist[d.dsem]
                    j = bisect.bisect_left(hist, (o.idx, -1)) - 1
                    val = hist[j][1]
                    key = ("d", d.dsem)
                else:
                    if d.eng == o.eng and d.eng == "pe" and o.dsem is None:
                        continue
                    val = d.count
                    key = ("e", d.eng)
                if val > need.get(key, 0):
                    need[key] = val
            ws = []
            for key, val in need.items():
                wk = (o.eng, key)
                if waited.get(wk, 0) >= val:
                    continue
                waited[wk] = val
                ws.append((key, val))
            o.waits = ws


class Kern:
    def __init__(self, nblk_p=8, nseq_s=4):
        self.nblk_p = nblk_p
        self.nseq_s = nseq_s
        self.TP = nblk_p * 512
        self.TS = nseq_s * 64
        self.s = Sched()
        self.rings = {}

    def ring(self, name, n):
        v = self.rings.get(name, 0)
        self.rings[name] = v + 1
        return v % n

    def dram_in(self, name, shape, dt=F32):
        return self.nc.dram_tensor(name, list(shape), dt, kind="ExternalInput").ap()

    def dram_out(self, name, shape, dt=F32):
        return self.nc.dram_tensor(name, list(shape), dt, kind="ExternalOutput").ap()

    def sb(self, name, shape, dt):
        return self.es.enter_context(self.nc.sbuf_tensor(name, list(shape), dt))

    def ps(self, name, shape, dt):
        return self.es.enter_context(self.nc.psum_tensor(name, list(shape), dt))

    def dma(self, q, out, in_, reads, writes, sem, **kw):
        self.s.op(q, lambda e: e.dma_start(out=out, in_=in_, **kw), reads, writes, dsem=sem)

    def mm(self, out, lhsT, rhs, start, stop, reads, writes):
        self.s.op("pe", lambda e: e.matmul(out, lhsT, rhs, start=start, stop=stop), reads, writes)

    def tr(self, out, in_, reads, writes):
        ident = self.ident
        self.s.op("pe", lambda e: e.transpose(out, in_, ident[:]), reads, writes)

    def act(self, out, in_, func, reads, writes, bias=None, scale=None, accum_out=None):
        kw = {}
        if bias is not None:
            kw["bias"] = bias
        if scale is not None:
            kw["scale"] = scale
        if accum_out is not None:
            kw["accum_out"] = accum_out
        self.s.op("act", lambda e: e.activation(out, in_, func, **kw), reads, writes)

    def tt(self, eng, out, in0, in1, op, reads, writes):
        self.s.op(eng, lambda e: e.tensor_tensor(out, in0, in1, op), reads, writes)

    def ts(self, eng, out, in0, s1, s2, op0, op1, reads, writes):
        if op1 is None:
            self.s.op(eng, lambda e: e.tensor_scalar(out, in0, s1, None, op0), reads, writes)
        else:
            self.s.op(eng, lambda e: e.tensor_scalar(out, in0, s1, s2, op0, op1), reads, writes)

    def stt(self, out, in0, scalar, in1, op0, op1, reads, writes):
        self.s.op("dve", lambda e: e.scalar_tensor_tensor(out, in0, scalar, in1, op0, op1), reads, writes)

    def newstat(self):
        c = self.ring("stat", 128)
        return self.stat[:, c:c + 1], ("stat", c)

    def build(self):
        nc = bass.Bass("TRN2", target_bir_lowering=False)
        self.nc = nc
        TP, TS = self.TP, self.TS
        self.x_p = self.dram_in("x_p", [TP, D])
        self.x_s = self.dram_in("x_s", [TS, D])
        self.st_hg = self.dram_in("st_hg", [self.nseq_s, 8, 128, 128])
        self.st_gl = self.dram_in("st_gl", [self.nseq_s, 4, 128, 256])
        self.w_in = self.dram_in("w_in", [D, NIN])
        self.w_out = self.dram_in("w_out", [D, D])
        self.w_gu = self.dram_in("w_gu", [D, 2 * DFF])
        self.w_dn = self.dram_in("w_dn", [DFF, D])
        self.w_gk2 = self.dram_in("w_gk2", [16, 512])
        self.c_lb = self.dram_in("c_lb", [128, 16])
        self.c_nmix = self.dram_in("c_nmix", [128, 16])
        self.c_nffn = self.dram_in("c_nffn", [128, 16])
        self.c_nfin = self.dram_in("c_nfin", [128, D])
        self.c_bgk = self.dram_in("c_bgk", [128, 4])
        self.c_hnw = self.dram_in("c_hnw", [128, 2, 512])
        self.c_mask = self.dram_in("c_mask", [128, 128])
        self.c_ident = self.dram_in("c_ident", [128, 128], BF16)
        self.c_sel = self.dram_in("c_sel", [128, 4])
        self.y_p = self.dram_out("y_p", [TP, D])
        self.y_s = self.dram_out("y_s", [TS, D])
        self.o_hg_p = self.dram_out("o_hg_p", [8, 128, 128])
        self.o_gl_p = self.dram_out("o_gl_p", [4, 128, 256])
        self.o_hg_s = self.dram_out("o_hg_s", [self.nseq_s, 8, 128, 128])
        self.o_gl_s = self.dram_out("o_gl_s", [self.nseq_s, 4, 128, 256])
        self.ws_in = nc.dram_tensor("ws_in", [D, NIN], BF16).ap()
        self.ws_out = nc.dram_tensor("ws_out", [D, D], BF16).ap()
        self.ws_gu = nc.dram_tensor("ws_gu", [D, 2 * DFF], BF16).ap()
        self.ws_dn = nc.dram_tensor("ws_dn", [DFF, D], BF16).ap()
        self.ag_src = nc.dram_tensor("ag_src", [128, 2048], F32).ap()
        self.ag_dst = nc.dram_tensor("ag_dst", [512, 2048], F32).ap()
        self.ag_src2 = nc.dram_tensor("ag_src2", [128, 64], F32).ap()
        self.ag_dst2 = nc.dram_tensor("ag_dst2", [512, 64], F32).ap()
        self.ag_src3 = nc.dram_tensor("ag_src3", [128, 64], F32).ap()
        self.ag_dst3 = nc.dram_tensor("ag_dst3", [512, 64], F32).ap()

        with contextlib.ExitStack() as es:
            self.es = es
            self.xres = self.sb("xres", [128, 4, D], F32)
            self.NST = 1
            self.stage = [self.sb(f"stage{i}", [128, D], F32) for i in range(self.NST)]
            self.xnb = [self.sb(f"xnb{i}", [128, D], BF16) for i in range(2)]
            self.hT = self.sb("hT", [128, KC, 512], BF16)
            self.ogT = self.sb("ogT", [128, KC, 512], BF16)
            self.NW = 3
            self.wt = [self.sb(f"wt{i}", [128, KC, 512], BF16) for i in range(self.NW)]
            self.vtok = self.sb("vtok", [128, 4, 512], BF16)
            self.sgtok = self.sb("sgtok", [128, 4, 512], BF16)
            self.qs = self.sb("qs", [128, 512], F32)
            self.fb = [self.sb(f"fb{i}", [128, 512], F32) for i in range(2)]
            self.kb = [self.sb(f"kb{i}", [128, 512], F32) for i in range(2)]
            self.cb = self.sb("cb", [128, 512], F32)
            self.e2 = self.sb("e2", [128, 512], F32)
            self.qt = [self.sb(f"qt{i}", [128, 512], BF16) for i in range(4)]
            self.kt = [self.sb(f"kt{i}", [128, 512], BF16) for i in range(4)]
            self.KT0 = self.sb("KT0", [128, 4, 128], BF16)
            self.KT1 = self.sb("KT1", [128, 4, 128], BF16)
            self.PTs = [self.sb(f"PTs{i}", [128, 4, 128], BF16) for i in range(3)]
            self.hstat = self.sb("hstat", [128, 32], F32)
            self.spbf = [self.sb(f"spbf{i}", [128, 256], BF16) for i in range(8)]
            self.ogbuf = self.sb("ogbuf", [128, 4, 512], BF16)
            self.junk = [self.sb(f"junk{i}", [128, 256], BF16) for i in range(2)]
            self.NACT = 8
            self.actT = self.sb("actT", [128, self.NACT, 512], BF16)
            self.Sbuf = self.sb("Sbuf", [128, D], F32)
            self.Sr = [self.sb(f"Sr{i}", [128, 256], F32) for i in range(2)]
            self.grT = self.sb("grT", [16, 512], F32)
            self.wgk2 = self.sb("wgk2", [16, 512], F32)
            self.nfin = self.sb("nfin", [128, D], F32)
            self.hnw = self.sb("hnw", [128, 2, 512], BF16)
            self.mask = self.sb("mask", [128, 128], F32)
            self.ident = self.sb("ident", [128, 128], BF16)
            self.stat = self.sb("stat", [128, 128], F32)
            self.cst = self.sb("cst", [128, 96], F32)
            self.tot = self.sb("tot", [128, 64], F32)
            self.dl = self.sb("dl", [128, 8], F32)
            self.Ach = self.sb("Ach", [128, 32], F32)
            self.bank = [self.ps(f"bank{i}", [128, 512], F32) for i in range(8)]
            self.esem = {e: es.enter_context(nc.semaphore(f"s_{e}")) for e in ["pe", "act", "dve", "pool"]}
            self.dsems = {}
            self.emit_program()
            self.s.finalize()
            for o in self.s.ops:
                if o.dsem is not None and o.dsem not in self.dsems:
                    self.dsems[o.dsem] = es.enter_context(nc.semaphore(f"d_{o.dsem}"))
            block = es.enter_context(nc.Block())
            self.emit_engines(block)
        return nc

    def emit_engines(self, block):
        per = {"pe": [], "act": [], "dve": [], "pool": [], "sp": []}
        for o in self.s.ops:
            per[o.eng].append(o)
        esem, dsems = self.esem, self.dsems

        def run(e, ops):
            for o in ops:
                for key, val in o.waits:
                    sem = dsems[key[1]] if key[0] == "d" else esem[key[1]]
                    e.wait_ge(sem, val)
                if o.fn is None:
                    continue
                ins = o.fn(e)
                if o.dsem is not None:
                    if o.dinc == 1:
                        ins.then_inc(dsems[o.dsem])
                    else:
                        ins.then_inc(dsems[o.dsem], o.dinc)
                elif o.marked:
                    ins.then_inc(esem[o.eng], 1)

        @block.tensor
        def _(e):
            run(e, per["pe"])

        @block.scalar
        def _(e):
            run(e, per["act"])

        @block.vector
        def _(e):
            run(e, per["dve"])

        @block.gpsimd
        def _(e):
            run(e, per["pool"])

        @block.sync
        def _(e):
            run(e, per["sp"])

    def emit_program(self):
        import os
        stop = os.environ.get("KSTOP", "all")
        self.emit_setup()
        self.zero_state()
        if stop != "setup" and self.nblk_p > 0:
            for b in range(self.nblk_p):
                self.emit_block(1, "p", b)
            if stop != "p1":
                if stop == "noxchg":
                    self.trickle_casts(len(self.cast_q))
                    self.zero_state()
                else:
                    self.emit_exchange()
                if stop != "xchg":
                    self.emit_stageA("p", 0)
                    for b in range(self.nblk_p):
                        if b + 1 < self.nblk_p:
                            hk = lambda b=b: self.emit_stageA("p", b + 1)
                        else:
                            hk = lambda: self.emit_stageA("s", 0)
                        self.emit_block(2, "p", b, do_A=False, hook=hk)
                    self.emit_prompt_state_out()
                    self.emit_block(2, "s", 0, do_A=False)
                    stop = "done"
        if stop in ("all", "noxchg"):
            self.emit_block(2, "s", 0)
        self.s.op("pool", None, reads=list(self.outkeys), writes=())

    def emit_setup(self):
        s = self.s
        self.outkeys = set()
        self.cast_q = []
        for name, src, dst, rows in (("in", self.w_in, self.ws_in, D), ("out", self.w_out, self.ws_out, D),
                                     ("gu", self.w_gu, self.ws_gu, D), ("dn", self.w_dn, self.ws_dn, DFF)):
            for r in range(rows // 128):
                for sub in range(4):
                    r0 = r * 128 + sub * 32
                    item = (dst[r0:r0 + 32, :], src[r0:r0 + 32, :], ("ws", name, r, sub), f"cast_{name}")
                    if name == "in":
                        self.emit_cast(item)
                    else:
                        self.cast_q.append(item)
        c = self.cst
        ld = lambda out, in_, key: self.dma("sp", out, in_, [], [key], "const")
        ld(c[:, 0:16], self.c_lb, ("cst", "lbl"))
        ld(c[:, 32:48], self.c_nmix, ("cst", "nmix"))
        ld(c[:, 48:64], self.c_nffn, ("cst", "nffn"))
        ld(c[:, 64:68], self.c_bgk, ("cst", "bgk"))
        ld(c[:, 72:76], self.c_sel, ("cst", "sel"))
        ld(self.nfin[:], self.c_nfin, ("nfin",))
        self.dma("pool", self.hnw[:], self.c_hnw, [], [("hnw",)], "const2")
        ld(self.mask[:], self.c_mask, ("mask",))
        ld(self.ident[:], self.c_ident, ("ident",))
        ld(self.wgk2[:], self.w_gk2, ("wgk2",))
        s.op("dve", lambda e: e.memset(c[:, 68:69], EPS), [], [("cst", "eps")])
        s.op("dve", lambda e: e.memset(c[:, 69:70], 1.0), [], [("cst", "one")])
        s.op("dve", lambda e: e.memset(c[:, 70:71], LNCQ), [], [("cst", "lncq")])
        s.op("dve", lambda e: e.memset(self.tot[:], 0.0), [], [("tot",)])
        s.op("pool", lambda e: e.memset(self.KT0[:], 0.0), [], [("KT",)])
        s.op("pool", lambda e: e.memset(self.KT1[:], 0.0), [], [("KT",)])
        self.tt("dve", c[:, 16:24], c[:, 0:8], c[:, 8:16], ALU.subtract, [("cst", "lbl")], [("cst", "lb")])
        self.act(c[:, 16:24], c[:, 16:24], AF.Sigmoid, [("cst", "lb")], [("cst", "lb")])
        self.ts("dve", c[:, 24:32], c[:, 16:24], -1.0, 1.0, ALU.mult, ALU.add, [("cst", "lb")], [("cst", "oml")])
        self.ts("dve", c[:, 64:68], c[:, 64:68], -1.0, None, ALU.mult, None, [("cst", "bgk")], [("cst", "bgk")])
        self.const_reads = [("cst", k) for k in ("lb", "oml", "nmix", "nffn", "bgk", "eps", "one", "lncq", "sel")]

    def emit_cast(self, item):
        dst, src, key, sem = item
        self.dma("pool", dst, src, [], [key], sem, max_dma_last_dim=4096)

    def trickle_casts(self, n):
        for _ in range(n):
            if self.cast_q:
                self.emit_cast(self.cast_q.pop(0))

    def zero_state(self):
        self.s.op("dve", lambda e: e.memset(self.Sbuf[:], 0.0), [], [("S", h) for h in range(12)])

    def wload(self, which, r0, nr, c0, ncol):
        slot = self.ring("w", self.NW)
        src = {"in": self.ws_in, "out": self.ws_out, "gu": self.ws_gu, "dn": self.ws_dn}[which]
        t = self.wt[slot]
        self.dma("sp", t[:, 0:nr, 0:ncol],
                 src[r0:r0 + nr * 128, c0:c0 + ncol].rearrange("(k p) n -> p k n", p=128),
                 [("ws", which, r_, sub) for r_ in range(r0 // 128, r0 // 128 + nr) for sub in range(4)],
                 [("w", slot)], f"w{slot}")
        return t, ("w", slot)

    def next_acc(self):
        a = self.ring("bank", 8)
        return self.bank[a], ("bank", a)

    def trm(self, out, in_, reads, writes):
        ident = self.ident
        self.s.op("pe", lambda e: e.matmul(out, in_, ident[:], start=True, stop=True), reads + [("ident",)], writes)

    def norm_transpose(self, src_ap, src_keys, nw_lo, dstT, dst_key, i):
        j = self.ring("xnb", 2)
        xn = self.xnb[j]
        ss, ssk = self.newstat()
        self.act(xn[:], src_ap, AF.Square, src_keys, [("xnb", j), ssk], accum_out=ss)
        rt, rtk = self.newstat()
        self.act(rt, ss, AF.Ln, [ssk, ("cst", "eps")], [rtk], bias=self.cst[:, 68:69], scale=1.0 / D)
        rs, rsk = self.newstat()
        self.act(rs, rt, AF.Exp, [rtk], [rsk], scale=-0.5)
        self.ts("dve", xn[:], src_ap, rs, None, ALU.mult, None, src_keys + [rsk], [("xnb", j)])
        for q in range(4):
            bk, bkk = self.next_acc()
            for r in range(4):
                kc = q * 4 + r
                self.trm(bk[:, r * 128:(r + 1) * 128], xn[:, kc * 128:(kc + 1) * 128], [("xnb", j)], [bkk])
            nwb = self.cst[:, nw_lo + q * 4: nw_lo + q * 4 + 4].unsqueeze(2).to_broadcast([128, 4, 128])
            self.tt("dve", dstT[:, q * 4:(q + 1) * 4, i * 128:(i + 1) * 128],
                    bk[:].rearrange("p (a b) -> p a b", a=4), nwb, ALU.mult,
                    [bkk, ("cst", "nmix"), ("cst", "nffn")], [dst_key(i)])
        return rs, rsk

    def emit_stageA(self, kind, b):
        T = 512 if kind == "p" else self.TS
        NT = T // 128
        xsrc = self.x_p if kind == "p" else self.x_s
        t0 = b * 512
        hTk = lambda i: ("hT", i)
        for i in range(NT):
            sidx = self.ring("stage", self.NST)
            st = self.stage[sidx]
            self.dma("sp", st[:], xsrc[t0 + i * 128: t0 + (i + 1) * 128, :], [], [("stage", sidx)], f"st{sidx}")
            self.norm_transpose(st[:], [("stage", sidx)], 32, self.hT, hTk, i)

    def emit_block(self, ph, kind, b, do_A=True, hook=None):
        T = 512 if kind == "p" else self.TS
        NT = T // 128
        xsrc = self.x_p if kind == "p" else self.x_s
        ydst = self.y_p if kind == "p" else self.y_s
        t0 = b * 512
        if do_A:
            self.emit_stageA(kind, b)
        if ph == 2:
            for i in range(NT):
                self.dma("sp", self.xres[:, i, :], xsrc[t0 + i * 128: t0 + (i + 1) * 128, :], [],
                         [("xres", i, g) for g in range(4)], f"xr{i}")
        hT_all = [("hT", i) for i in range(NT)]
        wtg, wk = self.wload("in", 0, KC, O_GR, 16)
        a, ak = self.next_acc()
        for kc in range(KC):
            self.mm(a[0:16, 0:T], wtg[:, kc, 0:16], self.hT[:, kc, 0:T], kc == 0, kc == KC - 1, [wk] + hT_all, [ak])
        self.s.op("dve", lambda e, a=a, T=T: e.tensor_copy(self.grT[:, 0:T], a[0:16, 0:T]), [ak], [("grT",)])
        groups = [
            dict(kind="hg", heads=[0, 1, 2, 3], vcol=O_HI, gcol=O_HGATE, qcol=O_HQ, kcol=O_HF, dv=128, hw=0, oc=0),
            dict(kind="hg", heads=[4, 5, 6, 7], vcol=O_HI + 512, gcol=O_HGATE + 512, qcol=O_HQ + 512, kcol=O_HF + 512, dv=128, hw=0, oc=512),
            dict(kind="gl", heads=[0, 1], vcol=O_GV, gcol=O_GGATE, qcol=O_GQ, kcol=O_GK, dv=256, hw=1, oc=1024),
            dict(kind="gl", heads=[2, 3], vcol=O_GV + 512, gcol=O_GGATE + 512, qcol=O_GQ + 256, kcol=O_GK + 256, dv=256, hw=1, oc=1536),
        ]
        for gi, g in enumerate(groups):
            self.emit_group(ph, kind, b, T, NT, g, gi, hT_all)
        if ph == 1:
            return
        import os
        dbg = os.environ.get("KDBG", "")
        if "noE" in dbg:
            return
        ogT_all = [("ogT", i) for i in range(NT)]
        for cg in range(4):
            wt, wk = self.wload("out", 0, KC, cg * 512, 512)
            for i in range(NT):
                a, ak = self.next_acc()
                for kc in range(KC):
                    self.mm(a[:], self.ogT[:, kc, i * 128:(i + 1) * 128], wt[:, kc, :], kc == 0, kc == KC - 1,
                            [wk, ("ogT", i)], [ak])
                xs = self.xres[:, i, cg * 512:(cg + 1) * 512]
                self.tt("dve", xs, a[:], xs, ALU.add, [ak, ("xres", i, cg)], [("xres", i, cg)])
        if "noF" in dbg:
            return
        for i in range(NT):
            self.norm_transpose(self.xres[:, i, :], [("xres", i, g_) for g_ in range(4)], 48, self.ogT,
                                lambda i_: ("ogT", i_), i)
        h2_all = [("ogT", i) for i in range(NT)]
        halves = [(0, 8), (8, 8), (16, 8), (24, 8), (32, 8), (40, 4)]
        for hidx_, (c0, nchk) in enumerate(halves):
            if hidx_ == 1 and hook is not None:
                hook()
            for fg in range(nchk // 4):
                cc0 = c0 + fg * 4
                wg, wgk = self.wload("gu", 0, KC, cc0 * 128, 512)
                wu, wuk = self.wload("gu", 0, KC, DFF + cc0 * 128, 512)
                for cb in range(4):
                    ag, agk = self.next_acc()
                    for kc in range(KC):
                        self.mm(ag[:, 0:T], wg[:, kc, cb * 128:(cb + 1) * 128], self.ogT[:, kc, 0:T], kc == 0,
                                kc == KC - 1, [wgk] + h2_all, [agk])
                    au, auk = self.next_acc()
                    for kc in range(KC):
                        self.mm(au[:, 0:T], wu[:, kc, cb * 128:(cb + 1) * 128], self.ogT[:, kc, 0:T], kc == 0,
                                kc == KC - 1, [wuk] + h2_all, [auk])
                    self.act(self.e2[:, 0:T], ag[:, 0:T], AF.Silu, [agk], [("e2",)])
                    lc = fg * 4 + cb
                    self.tt("dve", self.actT[:, lc, 0:T], self.e2[:, 0:T], au[:, 0:T], ALU.mult,
                            [("e2",), auk], [("actT", lc)])
            npieces = 1
            pk = nchk // npieces
            for cg in range(4):
                accs = [self.next_acc() for _ in range(NT)]
                for pc in range(npieces):
                    wd, wdk = self.wload("dn", (c0 + pc * pk) * 128, pk, cg * 512, 512)
                    for kk in range(pk):
                        lc = pc * pk + kk
                        for i in range(NT):
                            a, ak = accs[i]
                            self.mm(a[:], self.actT[:, lc, i * 128:(i + 1) * 128], wd[:, kk, :], lc == 0,
                                    lc == nchk - 1, [wdk, ("actT", lc)], [ak])
                for i in range(NT):
                    a, ak = accs[i]
                    xs = self.xres[:, i, cg * 512:(cg + 1) * 512]
                    self.tt("dve", xs, a[:], xs, ALU.add, [ak, ("xres", i, cg)], [("xres", i, cg)])
        if "noI" in dbg:
            return
        for i in range(NT):
            xk = [("xres", i, g_) for g_ in range(4)]
            j = self.ring("xnb", 2)
            ss, ssk = self.newstat()
            self.act(self.xnb[j][:], self.xres[:, i, :], AF.Square, xk, [("xnb", j), ssk], accum_out=ss)
            rt, rtk = self.newstat()
            self.act(rt, ss, AF.Ln, [ssk, ("cst", "eps")], [rtk], bias=self.cst[:, 68:69], scale=1.0 / D)
            rs, rsk = self.newstat()
            self.act(rs, rt, AF.Exp, [rtk], [rsk], scale=-0.5)
            self.stt(self.xres[:, i, :], self.xres[:, i, :], rs, self.nfin[:], ALU.mult, ALU.mult,
                     xk + [rsk, ("nfin",)], xk)
            ok = ("y", kind, b, i)
            self.dma("pool", ydst[t0 + i * 128: t0 + (i + 1) * 128, :], self.xres[:, i, :], xk, [ok] + xk, f"yout{i}")
            self.outkeys.add(ok)

    def emit_group(self, ph, kind, b, T, NT, g, gi, hT_all):
        dv = g["dv"]
        is_hg = g["kind"] == "hg"
        gs = 1.0 if is_hg else -1.0 / 16.0
        nch = T // CH
        heads = g["heads"]
        nh = len(heads)
        c = self.cst
        wv, wvk = self.wload("in", 0, KC, g["vcol"], 512)
        for i in range(NT):
            a, ak = self.next_acc()
            for kc in range(KC):
                self.mm(a[:], self.hT[:, kc, i * 128:(i + 1) * 128], wv[:, kc, :], kc == 0, kc == KC - 1,
                        [wvk, ("hT", i)], [ak])
            self.act(self.vtok[:, i, :], a[:], AF.Copy, [ak], [("vtok", i)])
        if ph == 2:
            wg, wgk = self.wload("in", 0, KC, g["gcol"], 512)
            for i in range(NT):
                a, ak = self.next_acc()
                for kc in range(KC):
                    self.mm(a[:], self.hT[:, kc, i * 128:(i + 1) * 128], wg[:, kc, :], kc == 0, kc == KC - 1,
                            [wgk, ("hT", i)], [ak])
                self.act(self.sgtok[:, i, :], a[:], AF.Silu, [ak], [("sgtok", i)])
                self.tt("pool", self.sgtok[:, i, :], self.sgtok[:, i, :], self.hnw[:, g["hw"], :], ALU.mult,
                        [("sgtok", i), ("hnw",)], [("sgtok", i)])
        ncolqk = 128 * nh
        if ph == 2:
            wq, wqk = self.wload("in", 0, KC, g["qcol"], ncolqk)
        wk_, wkk = self.wload("in", 0, KC, g["kcol"], ncolqk)
        def early(hi_, h):
            p = hi_ % 2
            fb, kb = self.fb[p], self.kb[p]
            fbk, kbk = ("fb", p), ("kb", p)
            a, ak = self.next_acc()
            for kc in range(KC):
                self.mm(a[:, 0:T], wk_[:, kc, hi_ * 128:(hi_ + 1) * 128], self.hT[:, kc, 0:T], kc == 0, kc == KC - 1,
                        [wkk] + hT_all, [ak])
            aqh = None
            if ph == 2:
                aq, aqk = self.next_acc()
                for kc in range(KC):
                    self.mm(aq[:, 0:T], wq[:, kc, hi_ * 128:(hi_ + 1) * 128], self.hT[:, kc, 0:T], kc == 0,
                            kc == KC - 1, [wqk] + hT_all, [aqk])
                aqh = (aq, aqk)
            if is_hg:
                self.act(fb[:, 0:T], a[:, 0:T], AF.Sigmoid, [ak], [fbk])
                self.ts("dve", fb[:, 0:T], fb[:, 0:T], c[:, 24 + h:25 + h], c[:, 16 + h:17 + h], ALU.mult,
                        ALU.add, [fbk, ("cst", "lb"), ("cst", "oml")], [fbk])
                self.ts("dve", kb[:, 0:T], fb[:, 0:T], -1.0, 1.0, ALU.mult, ALU.add, [fbk], [kbk])
                self.act(fb[:, 0:T], fb[:, 0:T], AF.Ln, [fbk], [fbk])
            else:
                self.act(kb[:, 0:T], a[:, 0:T], AF.Copy, [ak], [kbk])
                a2, a2k = self.next_acc()
                self.mm(a2[:, 0:T], self.wgk2[:, h * 128:(h + 1) * 128], self.grT[:, 0:T], True, True,
                        [("wgk2",), ("grT",)], [a2k])
                self.act(fb[:, 0:T], a2[:, 0:T], AF.Exp, [a2k, ("cst", "bgk")], [fbk],
                         bias=c[:, 64 + h:65 + h], scale=-1.0)
                self.act(fb[:, 0:T], fb[:, 0:T], AF.Ln, [fbk, ("cst", "one")], [fbk],
                         bias=c[:, 69:70], scale=1.0)
            return aqh

        def late(hi_, h, aqh):
            p = hi_ % 2
            fb, kb = self.fb[p], self.kb[p]
            fbk, kbk = ("fb", p), ("kb", p)
            hidx = h if is_hg else 8 + h
            qt, kt = self.qt[hi_], self.kt[hi_]
            Ach = self.Ach[:, hi_ * 8: hi_ * 8 + 8]
            Achk = ("Ach", hi_)
            self.s.op("dve", lambda e, T=T, fb=fb: e.tensor_tensor_scan(self.cb[:, 0:T], self.cst[:, 69:70].to_broadcast([128, T]),
                                                                         fb[:, 0:T], 0.0, ALU.mult, ALU.add),
                      [fbk, ("cst", "one")], [("cb",)])
            cb3 = self.cb[:, 0:T].rearrange("p (c t) -> p c t", t=CH)
            fb3 = fb[:, 0:T].rearrange("p (c t) -> p c t", t=CH)
            lastv = cb3[:, :, CH - 1]
            self.tt("dve", fb3, cb3, cb3[:, :, CH - 1:CH].to_broadcast([128, nch, CH]), ALU.subtract,
                    [("cb",)], [fbk])
            dl = self.dl
            self.s.op("dve", lambda e, lastv=lastv, dl=dl: e.tensor_copy(dl[:, 0:1], lastv[:, 0:1]), [("cb",)], [("dl",)])
            if nch > 1:
                self.tt("dve", dl[:, 1:nch], lastv[:, 1:nch], lastv[:, 0:nch - 1], ALU.subtract, [("cb",)], [("dl",)])
            self.act(Ach[:, 0:nch], dl[:, 0:nch], AF.Exp, [("dl",)], [Achk], scale=gs)
            if ph == 1:
                self.tt("dve", self.tot[:, hidx:hidx + 1], self.tot[:, hidx:hidx + 1], lastv[:, nch - 1:nch], ALU.add,
                        [("cb",), ("tot",)], [("tot",)])
            if ph == 2:
                self.act(self.e2[:, 0:T], fb[:, 0:T], AF.Exp, [fbk, ("cst", "lncq")], [("e2",)],
                         bias=c[:, 70:71], scale=gs)
            self.act(fb[:, 0:T], fb[:, 0:T], AF.Exp, [fbk], [fbk], scale=-gs)
            self.tt("pool", kt[:, 0:T], kb[:, 0:T], fb[:, 0:T], ALU.mult, [kbk, fbk], [("kt", hi_)])
            if ph == 1:
                self.trickle_casts(4)
            if ph == 2:
                aq, aqk = aqh
                self.act(self.qs[:, 0:T], aq[:, 0:T], AF.Silu if is_hg else AF.Copy, [aqk], [("qs",)])
                self.tt("pool", qt[:, 0:T], self.qs[:, 0:T], self.e2[:, 0:T], ALU.mult, [("qs",), ("e2",)], [("qt", hi_)])

        pend = None
        for hi_, h in enumerate(heads):
            aqh = early(hi_, h)
            if pend is not None:
                late(*pend)
            pend = (hi_, h, aqh)
        late(*pend)
        hpb = 512 // (2 * dv)

        def stage_P(i):
            tsl = slice(i * 128, (i + 1) * 128)
            bx, bxk = self.next_acc()
            for hi_ in range(nh):
                self.trm(bx[:, hi_ * 128:(hi_ + 1) * 128], self.kt[hi_][:, tsl], [("kt", hi_)], [bxk])
            if ph == 2:
                bs, bsk = self.next_acc()
                for hi_ in range(nh):
                    self.mm(bs[:, hi_ * 128:(hi_ + 1) * 128], self.kt[hi_][:, tsl], self.qt[hi_][:, tsl], True, True,
                            [("kt", hi_), ("qt", hi_)], [bsk])
            w = nh * 128
            self.act(self.KT0[0:64, 0:nh, :], bx[0:64, 0:w].rearrange("p (a b) -> p a b", a=nh), AF.Copy, [bxk], [("KT",)])
            self.act(self.KT1[64:128, 0:nh, :], bx[64:128, 0:w].rearrange("p (a b) -> p a b", a=nh), AF.Copy, [bxk],
                     [("KT",)])
            pr = None
            if ph == 2:
                pr = self.ring("PTs", 3)
                self.tt("dve", self.PTs[pr][:, 0:nh, :], bs[:, 0:w].rearrange("p (a b) -> p a b", a=nh),
                        self.mask[:].unsqueeze(1).to_broadcast([128, nh, 128]), ALU.mult, [bsk, ("mask",)], [("PTs", pr)])
            Us = {}
            bu = buk = None
            for hi_ in range(nh):
                vc = hi_ * dv
                if hi_ % hpb == 0:
                    bu, buk = self.next_acc()
                off = (hi_ % hpb) * 2 * dv
                for cc in range(2):
                    KTc = self.KT0 if cc == 0 else self.KT1
                    self.mm(bu[:, off + cc * dv: off + (cc + 1) * dv], KTc[:, hi_, :], self.vtok[:, i, vc:vc + dv],
                            True, True, [("KT",), ("vtok", i)], [buk])
                    Us[(hi_, cc)] = (bu[:, off + cc * dv: off + (cc + 1) * dv], buk)
            return (i, Us, pr)

        def stage_Q(i, Us, pr):
            spss = {}
            for hi_, h in enumerate(heads):
                hidx = h if is_hg else 8 + h
                scol = h * 128 if is_hg else 1024 + h * 256
                Achk = ("Ach", hi_)
                sps = []
                for cc in range(2):
                    cidx = i * 2 + cc
                    U, buk = Us[(hi_, cc)]
                    if kind == "p":
                        S_ap = self.Sbuf[:, scol:scol + dv]
                        Sk = ("S", hidx)
                    else:
                        sr = self.ring("Sr", 2)
                        S_ap = self.Sr[sr][:, 0:dv]
                        Sk = ("Sr", sr)
                        seq = cidx
                        src = self.st_hg[seq, h] if is_hg else self.st_gl[seq, h]
                        self.dma("sp", S_ap, src, [], [Sk], f"sr{sr}")
                    Acol = self.Ach[:, hi_ * 8 + cidx: hi_ * 8 + cidx + 1]
                    if ph == 2:
                        spr = self.ring("spbf", 8)
                        sp_ = self.spbf[spr][:, 0:dv]
                        self.act(sp_, S_ap, AF.Copy, [Sk, Achk], [("spbf", spr)], scale=Acol)
                        sps.append((sp_, ("spbf", spr)))
                    self.stt(S_ap, S_ap, Acol, U, ALU.mult, ALU.add, [Sk, Achk, buk], [Sk])
                    if kind == "s" and ph == 2:
                        dst = self.o_hg_s[seq, h] if is_hg else self.o_gl_s[seq, h]
                        ok = ("so", seq, hidx)
                        self.dma("pool", dst, S_ap, [Sk], [ok, Sk], f"sout{sr}")
                        self.outkeys.add(ok)
                spss[hi_] = sps
            if ph == 2:
                bo, bok = self.next_acc()
                for hi_ in range(nh):
                    vc = hi_ * dv
                    qt = self.qt[hi_]
                    self.mm(bo[:, vc:vc + dv], self.PTs[pr][:, hi_, :], self.vtok[:, i, vc:vc + dv], True, False,
                            [("PTs", pr), ("vtok", i)], [bok])
                    for cc in range(2):
                        sp_, spk = spss[hi_][cc]
                        p0 = cc * 64
                        self.mm(bo[p0:p0 + 64, vc:vc + dv], qt[:, i * 128 + p0: i * 128 + p0 + 64], sp_, False, True,
                                [("qt", hi_), spk], [bok])
                c0 = self.ring("hstat", 8) * 4
                hsb = self.hstat[:, c0:c0 + nh]
                hskeys = [("hstat", c0 + j) for j in range(nh)]
                for hi_ in range(nh):
                    vc = hi_ * dv
                    jk = self.ring("junk", 2)
                    self.act(self.junk[jk][:, 0:dv], bo[:, vc:vc + dv], AF.Square, [bok], [("junk", jk), hskeys[hi_]],
                             accum_out=self.hstat[:, c0 + hi_:c0 + hi_ + 1])
                self.act(hsb, hsb, AF.Ln, hskeys + [("cst", "eps")], hskeys, bias=c[:, 68:69], scale=1.0 / dv)
                self.act(hsb, hsb, AF.Exp, hskeys, hskeys, scale=-0.5)
                for hi_ in range(nh):
                    vc = hi_ * dv
                    self.stt(self.ogbuf[:, i, vc:vc + dv], bo[:, vc:vc + dv], self.hstat[:, c0 + hi_:c0 + hi_ + 1],
                             self.sgtok[:, i, vc:vc + dv], ALU.mult, ALU.mult,
                             [bok, hskeys[hi_], ("sgtok", i)], [("ogbuf", i)])

        pendP = None
        for i in range(NT):
            cur = stage_P(i)
            if pendP is not None:
                stage_Q(*pendP)
            pendP = cur
        stage_Q(*pendP)
        if ph == 2:
            for i in range(NT):
                bk, bkk = self.next_acc()
                for rr in range(4):
                    self.trm(bk[:, rr * 128:(rr + 1) * 128], self.ogbuf[:, i, rr * 128:(rr + 1) * 128],
                             [("ogbuf", i)], [bkk])
                q = g["oc"] // 512
                self.act(self.ogT[:, q * 4:(q + 1) * 4, i * 128:(i + 1) * 128],
                         bk[:].rearrange("p (a b) -> p a b", a=4), AF.Copy, [bkk], [("ogT", i)])

    def emit_exchange(self):
        s = self.s
        self.trickle_casts(len(self.cast_q))
        Skeys = [("S", h) for h in range(12)]
        self.dma("pool", self.ag_src[:, :], self.Sbuf[:], Skeys, [("agsrc",)], "ag1")
        self.dma("pool", self.ag_src2[:, :], self.tot[:], [("tot",)], [("agsrc2",)], "ag2")
        s.op("pool", lambda e: e.collective_compute("AllGather", ALU.bypass, replica_groups=[[0, 1, 2, 3], [4, 5, 6, 7]],
                                                    ins=[self.ag_src], outs=[self.ag_dst]),
             [("agsrc",)], [("agdst",)], dsem="agcc", dinc=1)
        s.op("pool", lambda e: e.collective_compute("AllGather", ALU.bypass, replica_groups=[[0, 1, 2, 3], [4, 5, 6, 7]],
                                                    ins=[self.ag_src2], outs=[self.ag_dst2]),
             [("agsrc2",), ("agdst",)], [("agdst2",)], dsem="agcc2", dinc=1)
        self.dma("pool", self.ag_src3[:, :], self.tot[:], [("tot",)], [("agsrc3",)], "ag3")
        s.op("pool", lambda e: e.collective_compute("AllGather", ALU.bypass, replica_groups=[[0, 1, 2, 3], [4, 5, 6, 7]],
                                                    ins=[self.ag_src3], outs=[self.ag_dst3]),
             [("agsrc3",), ("agdst",), ("agdst2",)], [("agdst",), ("agdst2",), ("agdst3",)], dsem="agcc3", dinc=1)
        R = self.xres[:, 0, :]
        L = self.xres[:, 1, :]
        Rk = [("xres", 0, g) for g in range(4)]
        Lk = [("xres", 1, g) for g in range(4)]
        c = self.cst
        s.op("dve", lambda e: e.memset(self.Sbuf[:], 0.0), [], Skeys)
        self.dma("sp", R, self.ag_dst[0:128, 0:D], [("agdst",)], Rk, "agl0")
        self.stt(self.Sbuf[:], R, c[:, 73:74], self.Sbuf[:], ALU.mult, ALU.add, Rk + Skeys + [("cst", "sel")], Skeys)
        for k in (1, 2):
            self.dma("sp", L, self.ag_dst[k * 128:(k + 1) * 128, 0:D], [("agdst",)], Lk, f"aglL{k}")
            self.dma("sp", self.tot[:, 0:16], self.ag_dst2[k * 128:(k + 1) * 128, 0:16], [("agdst2",)], [("tot",)],
                     f"aglT{k}")
            self.act(c[:, 76:84], self.tot[:, 0:8], AF.Exp, [("tot",)], [("cst", "atot")], scale=1.0)
            self.act(c[:, 84:88], self.tot[:, 8:12], AF.Exp, [("tot",)], [("cst", "atot")], scale=-1.0 / 16.0)
            for h in range(12):
                sc, dv = (h * 128, 128) if h < 8 else (1024 + (h - 8) * 256, 256)
                self.stt(R[:, sc:sc + dv], R[:, sc:sc + dv], c[:, 76 + h:77 + h], L[:, sc:sc + dv], ALU.mult, ALU.add,
                         Rk + Lk + [("cst", "atot")], Rk)
            self.stt(self.Sbuf[:], R, c[:, 73 + k:74 + k], self.Sbuf[:], ALU.mult, ALU.add,
                     Rk + Skeys + [("cst", "sel")], Skeys)

    def emit_prompt_state_out(self):
        Skeys = [("S", h) for h in range(12)]
        ok = ("spo",)
        self.dma("pool", self.o_hg_p.rearrange("h k v -> k h v"),
                 self.Sbuf[:, 0:1024].rearrange("k (h v) -> k h v", h=8), Skeys, [ok], "sout")
        self.dma("pool", self.o_gl_p.rearrange("h k v -> k h v"),
                 self.Sbuf[:, 1024:2048].rearrange("k (h v) -> k h v", h=4), Skeys, [ok], "sout")
        self.outkeys.add(ok)


def _build(nblk_p=8, nseq_s=4):
    k = Kern(nblk_p, nseq_s)
    return k


def make_inputs(inputs, nblk_p=8, nseq_s=4, ncores=8):
    f32 = np.float32
    xp = np.asarray(inputs["x_prompt"], f32)
    xs = np.asarray(inputs["x_sample"], f32)
    shg = np.asarray(inputs["state_hgrn"], f32)[0]
    sgl = np.asarray(inputs["state_gla"], f32)[0]
    TP = nblk_p * 512
    lbl = np.asarray(inputs["lb_logits"], f32)
    c_lb = np.ascontiguousarray(lbl.reshape(2, 8, 128).transpose(2, 0, 1).reshape(128, 16))
    feat = lambda v: np.ascontiguousarray(np.asarray(v, f32).reshape(-1, 128).T)
    c_nmix = feat(inputs["norm_mix"][0])
    c_nffn = feat(inputs["norm_ffn"][0])
    c_nfin = np.ascontiguousarray(np.broadcast_to(np.asarray(inputs["norm_final"], f32)[None, :], (128, D)))
    c_bgk = feat(inputs["b_gk"][0])
    hgn = np.asarray(inputs["hg_norm"], f32)[0]
    gln = np.asarray(inputs["gla_norm"], f32)[0]
    c_hnw = np.ascontiguousarray(np.broadcast_to(
        np.stack([np.tile(hgn, 4), np.tile(gln, 2)])[None], (128, 2, 512)))
    idx = np.arange(128)
    mask = ((idx[:, None] // 64 == idx[None, :] // 64) & (idx[:, None] <= idx[None, :])).astype(f32)
    ident = np.eye(128, dtype=f32).astype(ml_dtypes.bfloat16)
    common = dict(
        w_in=np.ascontiguousarray(np.asarray(inputs["w_in"], f32)[0]),
        w_out=np.ascontiguousarray(np.asarray(inputs["w_out"], f32)[0]),
        w_gu=np.ascontiguousarray(np.asarray(inputs["w_gate_up"], f32)[0]),
        w_dn=np.ascontiguousarray(np.asarray(inputs["w_down"], f32)[0]),
        w_gk2=np.ascontiguousarray(np.asarray(inputs["w_gk2"], f32)[0]),
        c_lb=c_lb, c_nmix=c_nmix, c_nffn=c_nffn, c_nfin=c_nfin, c_bgk=c_bgk, c_hnw=c_hnw,
        c_mask=mask, c_ident=ident,
    )
    maps = []
    for c in range(ncores):
        bseq, seg = c // 4, c % 4
        sel = np.zeros((128, 4), f32)
        sel[:, seg] = 1.0
        m = dict(common)
        m["x_p"] = np.ascontiguousarray(xp[bseq, seg * TP:(seg + 1) * TP])
        m["x_s"] = np.ascontiguousarray(xs[c * nseq_s:(c + 1) * nseq_s].reshape(nseq_s * 64, D))
        m["st_hg"] = np.ascontiguousarray(shg[c * nseq_s:(c + 1) * nseq_s])
        m["st_gl"] = np.ascontiguousarray(sgl[c * nseq_s:(c + 1) * nseq_s])
        m["c_sel"] = sel
        maps.append(m)
    return maps


_NC_CACHE = {}


def run(inputs, nblk_p=8, nseq_s=4):
    key = (nblk_p, nseq_s)
    kern = Kern(nblk_p, nseq_s)
    nc = kern.build()
    maps = make_inputs(inputs, nblk_p, nseq_s)
    res = run_bass_kernel_spmd(nc, maps, core_ids=list(range(8)))
    R = res.results
    TP = nblk_p * 512
    y_p = np.stack([np.concatenate([R[b * 4 + s]["y_p"] for s in range(4)], axis=0) for b in range(2)])
    y_s = np.concatenate([R[c]["y_s"].reshape(nseq_s, 64, D) for c in range(8)], axis=0)
    hg_p = np.stack([R[3]["o_hg_p"], R[7]["o_hg_p"]])[None]
    gl_p = np.stack([R[3]["o_gl_p"], R[7]["o_gl_p"]])[None]
    hg_s = np.concatenate([R[c]["o_hg_s"] for c in range(8)], axis=0)[None]
    gl_s = np.concatenate([R[c]["o_gl_s"] for c in range(8)], axis=0)[None]
    f = lambda a: np.ascontiguousarray(a, dtype=np.float32)
    return (f(y_p), f(y_s), f(hg_p), f(gl_p), f(hg_s), f(gl_s))


def kernel(**inputs):
    return run(inputs, 8, 4)
```

```python
import contextlib
import os
DBG = os.environ.get('KDBG', '')
import math
import numpy as np
import ml_dtypes
import concourse.bass as bass
import concourse.mybir as mybir
from concourse.bass_utils import run_bass_kernel_spmd

F32 = mybir.dt.float32
BF16 = mybir.dt.bfloat16
AF = mybir.ActivationFunctionType
ALU = mybir.AluOpType

D = 2048
KC = 16
NIN = 7184
DFF = 5632
EPS = 1e-6
CH = 64
LNCQ = math.log(128.0 ** -0.5)

O_HQ, O_HF, O_HI, O_HGATE, O_GQ, O_GK, O_GV, O_GGATE, O_GR = 0, 1024, 2048, 3072, 4096, 4608, 5120, 6144, 7168


class Op:
    __slots__ = ("eng", "fn", "deps", "dsem", "idx", "waits", "marked", "count", "dcount", "dinc")


class Sched:
    def __init__(self):
        self.ops = []
        self.last_w = {}
        self.readers = {}

    def op(self, eng, fn, reads=(), writes=(), dsem=None, dinc=16):
        o = Op()
        o.eng, o.fn, o.dsem, o.idx = eng, fn, dsem, len(self.ops)
        o.dinc = dinc
        o.marked = False
        deps = set()
        for k in reads:
            w = self.last_w.get(k)
            if w is not None:
                deps.add(w)
        for k in writes:
            w = self.last_w.get(k)
            if w is not None:
                deps.add(w)
            r = self.readers.get(k)
            if r:
                deps.update(r)
        for k in reads:
            rl = self.readers.setdefault(k, [])
            if dsem is None:
                for j in range(len(rl)):
                    if rl[j].dsem is None and rl[j].eng == eng:
                        rl[j] = o
                        break
                else:
                    rl.append(o)
            else:
                rl.append(o)
        for k in writes:
            self.last_w[k] = o
            self.readers[k] = []
        deps.discard(o)
        o.deps = deps
        self.ops.append(o)
        return o

    def finalize(self):
        for o in self.ops:
            for d in o.deps:
                if d.dsem is None:
                    if d.eng == o.eng and d.eng == "pe" and o.dsem is None:
                        continue
                    d.marked = True
        cnt = {}
        dcnt = {}
        dma_hist = {}
        for o in self.ops:
            if o.dsem is not None:
                dcnt[o.dsem] = dcnt.get(o.dsem, 0) + o.dinc
                o.dcount = dcnt[o.dsem]
                dma_hist.setdefault(o.dsem, []).append((o.idx, o.dcount))
            elif o.marked:
                cnt[o.eng] = cnt.get(o.eng, 0) + 1
                o.count = cnt[o.eng]
        waited = {}
        import bisect
        for o in self.ops:
            need = {}
            for d in o.deps:
                if d.dsem is not None:
                    hist = dma_hist[d.dsem]
                    j = bisect.bisect_left(hist, (o.idx, -1)) - 1
                    val = hist[j][1]
                    key = ("d", d.dsem)
                else:
                    if d.eng == o.eng and d.eng == "pe" and o.dsem is None:
                        continue
                    val = d.count
                    key = ("e", d.eng)
                if val > need.get(key, 0):
                    need[key] = val
            ws = []
            for key, val in need.items():
                wk = (o.eng, key)
                if waited.get(wk, 0) >= val:
                    continue
                waited[wk] = val
                ws.append((key, val))
            o.waits = ws


class Kern:
    def __init__(self, nblk_p=8, nseq_s=4):
        self.nblk_p = nblk_p
        self.nseq_s = nseq_s
        self.TP = nblk_p * 512
        self.TS = nseq_s * 64
        self.s = Sched()
        self.rings = {}

    def ring(self, name, n):
        v = self.rings.get(name, 0)
        self.rings[name] = v + 1
        return v % n

    def dram_in(self, name, shape, dt=F32):
        return self.nc.dram_tensor(name, list(shape), dt, kind="ExternalInput").ap()

    def dram_out(self, name, shape, dt=F32):
        return self.nc.dram_tensor(name, list(shape), dt, kind="ExternalOutput").ap()

    def sb(self, name, shape, dt):
        return self.es.enter_context(self.nc.sbuf_tensor(name, list(shape), dt))

    def ps(self, name, shape, dt):
        return self.es.enter_context(self.nc.psum_tensor(name, list(shape), dt))

    def dma(self, q, out, in_, reads, writes, sem, **kw):
        self.s.op(q, lambda e: e.dma_start(out=out, in_=in_, **kw), reads, writes, dsem=sem)

    def mm(self, out, lhsT, rhs, start, stop, reads, writes):
        self.s.op("pe", lambda e: e.matmul(out, lhsT, rhs, start=start, stop=stop), reads, writes)

    def tr(self, out, in_, reads, writes):
        ident = self.ident
        self.s.op("pe", lambda e: e.transpose(out, in_, ident[:]), reads, writes)

    def act(self, out, in_, func, reads, writes, bias=None, scale=None, accum_out=None):
        kw = {}
        if bias is not None:
            kw["bias"] = bias
        if scale is not None:
            kw["scale"] = scale
        if accum_out is not None:
            kw["accum_out"] = accum_out
        self.s.op("act", lambda e: e.activation(out, in_, func, **kw), reads, writes)

    def tt(self, eng, out, in0, in1, op, reads, writes):
        self.s.op(eng, lambda e: e.tensor_tensor(out, in0, in1, op), reads, writes)

    def ts(self, eng, out, in0, s1, s2, op0, op1, reads, writes):
        if op1 is None:
            self.s.op(eng, lambda e: e.tensor_scalar(out, in0, s1, None, op0), reads, writes)
        else:
            self.s.op(eng, lambda e: e.tensor_scalar(out, in0, s1, s2, op0, op1), reads, writes)

    def stt(self, out, in0, scalar, in1, op0, op1, reads, writes):
        self.s.op("dve", lambda e: e.scalar_tensor_tensor(out, in0, scalar, in1, op0, op1), reads, writes)

    def newstat(self):
        c = self.ring("stat", 128)
        return self.stat[:, c:c + 1], ("stat", c)

    def build(self):
        nc = bass.Bass("TRN2", target_bir_lowering=False)
        self.nc = nc
        TP, TS = self.TP, self.TS
        self.x_p = self.dram_in("x_p", [TP, D])
        self.x_s = self.dram_in("x_s", [TS, D])
        self.st_hg = self.dram_in("st_hg", [self.nseq_s, 8, 128, 128])
        self.st_gl = self.dram_in("st_gl", [self.nseq_s, 4, 128, 256])
        self.w_in = self.dram_in("w_in", [D, NIN])
        self.w_out = self.dram_in("w_out", [D, D])
        self.w_gu = self.dram_in("w_gu", [D, 2 * DFF])
        self.w_dn = self.dram_in("w_dn", [DFF, D])
        self.w_gk2 = self.dram_in("w_gk2", [16, 512])
        self.c_lb = self.dram_in("c_lb", [128, 16])
        self.c_nmix = self.dram_in("c_nmix", [128, 16])
        self.c_nffn = self.dram_in("c_nffn", [128, 16])
        self.c_nfin = self.dram_in("c_nfin", [128, D])
        self.c_bgk = self.dram_in("c_bgk", [128, 4])
        self.c_hnw = self.dram_in("c_hnw", [128, 2, 512])
        self.c_mask = self.dram_in("c_mask", [128, 128])
        self.c_ident = self.dram_in("c_ident", [128, 128], BF16)
        self.c_sel = self.dram_in("c_sel", [128, 4])
        self.y_p = self.dram_out("y_p", [TP, D])
        self.y_s = self.dram_out("y_s", [TS, D])
        self.o_hg_p = self.dram_out("o_hg_p", [8, 128, 128])
        self.o_gl_p = self.dram_out("o_gl_p", [4, 128, 256])
        self.o_hg_s = self.dram_out("o_hg_s", [self.nseq_s, 8, 128, 128])
        self.o_gl_s = self.dram_out("o_gl_s", [self.nseq_s, 4, 128, 256])
        self.ws_in = nc.dram_tensor("ws_in", [D, NIN], BF16).ap()
        self.ws_out = nc.dram_tensor("ws_out", [D, D], BF16).ap()
        self.ws_gu = nc.dram_tensor("ws_gu", [D, 2 * DFF], BF16).ap()
        self.ws_dn = nc.dram_tensor("ws_dn", [DFF, D], BF16).ap()
        self.ag_src = nc.dram_tensor("ag_src", [128, 2048], F32).ap()
        self.ag_dst = nc.dram_tensor("ag_dst", [512, 2048], F32).ap()
        self.ag_src2 = nc.dram_tensor("ag_src2", [128, 64], F32).ap()
        self.ag_dst2 = nc.dram_tensor("ag_dst2", [512, 64], F32).ap()
        self.ag_src3 = nc.dram_tensor("ag_src3", [128, 64], F32).ap()
        self.ag_dst3 = nc.dram_tensor("ag_dst3", [512, 64], F32).ap()

        with contextlib.ExitStack() as es:
            self.es = es
            self.xres = self.sb("xres", [128, 4, D], F32)
            self.NST = 1
            self.stage = [self.sb(f"stage{i}", [128, D], F32) for i in range(self.NST)]
            self.xnb = [self.sb(f"xnb{i}", [128, D], BF16) for i in range(2)]
            self.hT = self.sb("hT", [128, KC, 512], BF16)
            self.ogT = self.sb("ogT", [128, KC, 512], BF16)
            self.NW = 3
            self.wt = [self.sb(f"wt{i}", [128, KC, 512], BF16) for i in range(self.NW)]
            self.vtok = self.sb("vtok", [128, 4, 512], BF16)
            self.sgtok = self.sb("sgtok", [128, 4, 512], BF16)
            self.qs = self.sb("qs", [128, 512], F32)
            self.fb = [self.sb(f"fb{i}", [128, 512], F32) for i in range(2)]
            self.kb = [self.sb(f"kb{i}", [128, 512], F32) for i in range(2)]
            self.cb = self.sb("cb", [128, 512], F32)
            self.e2 = self.sb("e2", [128, 512], F32)
            self.qt = [self.sb(f"qt{i}", [128, 512], BF16) for i in range(4)]
            self.kt = [self.sb(f"kt{i}", [128, 512], BF16) for i in range(4)]
            self.KT0 = self.sb("KT0", [128, 4, 128], BF16)
            self.KT1 = self.sb("KT1", [128, 4, 128], BF16)
            self.PTs = [self.sb(f"PTs{i}", [128, 4, 128], BF16) for i in range(3)]
            self.hstat = self.sb("hstat", [128, 32], F32)
            self.spbf = [self.sb(f"spbf{i}", [128, 256], BF16) for i in range(8)]
            self.ogbuf = self.sb("ogbuf", [128, 4, 512], BF16)
            self.junk = [self.sb(f"junk{i}", [128, 256], BF16) for i in range(2)]
            self.NACT = 8
            self.actT = self.sb("actT", [128, self.NACT, 512], BF16)
            self.Sbuf = self.sb("Sbuf", [128, D], F32)
            self.Sr = [self.sb(f"Sr{i}", [128, 256], F32) for i in range(2)]
            self.grT = self.sb("grT", [16, 512], F32)
            self.wgk2 = self.sb("wgk2", [16, 512], F32)
            self.nfin = self.sb("nfin", [128, D], F32)
            self.hnw = self.sb("hnw", [128, 2, 512], BF16)
            self.mask = self.sb("mask", [128, 128], F32)
            self.ident = self.sb("ident", [128, 128], BF16)
            self.stat = self.sb("stat", [128, 128], F32)
            self.cst = self.sb("cst", [128, 96], F32)
            self.tot = self.sb("tot", [128, 64], F32)
            self.dl = self.sb("dl", [128, 8], F32)
            self.Ach = self.sb("Ach", [128, 32], F32)
            self.bank = [self.ps(f"bank{i}", [128, 512], F32) for i in range(8)]
            self.esem = {e: es.enter_context(nc.semaphore(f"s_{e}")) for e in ["pe", "act", "dve", "pool"]}
            self.dsems = {}
            self.emit_program()
            self.s.finalize()
            for o in self.s.ops:
                if o.dsem is not None and o.dsem not in self.dsems:
                    self.dsems[o.dsem] = es.enter_context(nc.semaphore(f"d_{o.dsem}"))
            block = es.enter_context(nc.Block())
            self.emit_engines(block)
        return nc

    def emit_engines(self, block):
        per = {"pe": [], "act": [], "dve": [], "pool": [], "sp": []}
        for o in self.s.ops:
            per[o.eng].append(o)
        esem, dsems = self.esem, self.dsems

        def run(e, ops):
            for o in ops:
                for key, val in o.waits:
                    sem = dsems[key[1]] if key[0] == "d" else esem[key[1]]
                    e.wait_ge(sem, val)
                if o.fn is None:
                    continue
                ins = o.fn(e)
                if o.dsem is not None:
                    if o.dinc == 1:
                        ins.then_inc(dsems[o.dsem])
                    else:
                        ins.then_inc(dsems[o.dsem], o.dinc)
                elif o.marked:
                    ins.then_inc(esem[o.eng], 1)

        @block.tensor
        def _(e):
            run(e, per["pe"])

        @block.scalar
        def _(e):
            run(e, per["act"])

        @block.vector
        def _(e):
            run(e, per["dve"])

        @block.gpsimd
        def _(e):
            run(e, per["pool"])

        @block.sync
        def _(e):
            run(e, per["sp"])

    def emit_program(self):
        import os
        stop = os.environ.get("KSTOP", "all")
        self.emit_setup()
        self.zero_state()
        if stop != "setup" and self.nblk_p > 0:
            for b in range(self.nblk_p):
                self.emit_block(1, "p", b)
            if stop != "p1":
                if stop == "noxchg":
                    self.trickle_casts(len(self.cast_q))
                    self.zero_state()
                else:
                    self.emit_exchange()
                    self.pending_combine = True
                if stop == "xchg":
                    self.emit_exchange_combine()
                if stop != "xchg":
                    self.emit_stageA("p", 0)
                    for b in range(self.nblk_p):
                        if b + 1 < self.nblk_p:
                            hk = lambda b=b: self.emit_stageA("p", b + 1)
                        else:
                            hk = lambda: self.emit_stageA("s", 0)
                        self.emit_block(2, "p", b, do_A=False, hook=hk)
                    self.emit_prompt_state_out()
                    self.emit_block(2, "s", 0, do_A=False)
                    stop = "done"
        if stop in ("all", "noxchg"):
            self.emit_block(2, "s", 0)
        self.s.op("pool", None, reads=list(self.outkeys), writes=())

    def emit_setup(self):
        s = self.s
        self.outkeys = set()
        self.cast_q = []
        for name, src, dst, rows in (("in", self.w_in, self.ws_in, D), ("out", self.w_out, self.ws_out, D),
                                     ("gu", self.w_gu, self.ws_gu, D), ("dn", self.w_dn, self.ws_dn, DFF)):
            for r in range(rows // 128):
                for sub in range(4):
                    r0 = r * 128 + sub * 32
                    item = (dst[r0:r0 + 32, :], src[r0:r0 + 32, :], ("ws", name, r, sub), f"cast_{name}")
                    if name == "in":
                        self.emit_cast(item)
                    else:
                        self.cast_q.append(item)
        c = self.cst
        ld = lambda out, in_, key: self.dma("sp", out, in_, [], [key], "const")
        ld(c[:, 0:16], self.c_lb, ("cst", "lbl"))
        ld(c[:, 32:48], self.c_nmix, ("cst", "nmix"))
        ld(c[:, 48:64], self.c_nffn, ("cst", "nffn"))
        ld(c[:, 64:68], self.c_bgk, ("cst", "bgk"))
        ld(c[:, 72:76], self.c_sel, ("cst", "sel"))
        ld(self.nfin[:], self.c_nfin, ("nfin",))
        self.dma("pool", self.hnw[:], self.c_hnw, [], [("hnw",)], "const2")
        ld(self.mask[:], self.c_mask, ("mask",))
        ld(self.ident[:], self.c_ident, ("ident",))
        ld(self.wgk2[:], self.w_gk2, ("wgk2",))
        s.op("dve", lambda e: e.memset(c[:, 68:69], EPS), [], [("cst", "eps")])
        s.op("dve", lambda e: e.memset(c[:, 69:70], 1.0), [], [("cst", "one")])
        s.op("dve", lambda e: e.memset(c[:, 70:71], LNCQ), [], [("cst", "lncq")])
        s.op("dve", lambda e: e.memset(self.tot[:], 0.0), [], [("tot",)])
        s.op("pool", lambda e: e.memset(self.KT0[:], 0.0), [], [("KT",)])
        s.op("pool", lambda e: e.memset(self.KT1[:], 0.0), [], [("KT",)])
        self.tt("dve", c[:, 16:24], c[:, 0:8], c[:, 8:16], ALU.subtract, [("cst", "lbl")], [("cst", "lb")])
        self.act(c[:, 16:24], c[:, 16:24], AF.Sigmoid, [("cst", "lb")], [("cst", "lb")])
        self.ts("dve", c[:, 24:32], c[:, 16:24], -1.0, 1.0, ALU.mult, ALU.add, [("cst", "lb")], [("cst", "oml")])
        self.ts("dve", c[:, 64:68], c[:, 64:68], -1.0, None, ALU.mult, None, [("cst", "bgk")], [("cst", "bgk")])
        self.const_reads = [("cst", k) for k in ("lb", "oml", "nmix", "nffn", "bgk", "eps", "one", "lncq", "sel")]

    def emit_cast(self, item):
        dst, src, key, sem = item
        self.dma("pool", dst, src, [], [key], sem, max_dma_last_dim=4096)

    def trickle_casts(self, n):
        for _ in range(n):
            if self.cast_q:
                self.emit_cast(self.cast_q.pop(0))

    def zero_state(self):
        self.s.op("dve", lambda e: e.memset(self.Sbuf[:], 0.0), [], [("S", h) for h in range(12)])

    def wload(self, which, r0, nr, c0, ncol):
        slot = self.ring("w", self.NW)
        src = {"in": self.ws_in, "out": self.ws_out, "gu": self.ws_gu, "dn": self.ws_dn}[which]
        t = self.wt[slot]
        self.dma("sp", t[:, 0:nr, 0:ncol],
                 src[r0:r0 + nr * 128, c0:c0 + ncol].rearrange("(k p) n -> p k n", p=128),
                 [("ws", which, r_, sub) for r_ in range(r0 // 128, r0 // 128 + nr) for sub in range(4)],
                 [("w", slot)], f"w{slot}")
        return t, ("w", slot)

    def next_acc(self):
        a = self.ring("bank", 8)
        return self.bank[a], ("bank", a)

    def trm(self, out, in_, reads, writes):
        ident = self.ident
        self.s.op("pe", lambda e: e.matmul(out, in_, ident[:], start=True, stop=True), reads + [("ident",)], writes)

    def norm_transpose(self, src_ap, src_keys, nw_lo, dstT, dst_key, i):
        j = self.ring("xnb", 2)
        xn = self.xnb[j]
        ss, ssk = self.newstat()
        self.act(xn[:], src_ap, AF.Square, src_keys, [("xnb", j), ssk], accum_out=ss)
        rt, rtk = self.newstat()
        self.act(rt, ss, AF.Ln, [ssk, ("cst", "eps")], [rtk], bias=self.cst[:, 68:69], scale=1.0 / D)
        rs, rsk = self.newstat()
        self.act(rs, rt, AF.Exp, [rtk], [rsk], scale=-0.5)
        self.ts("dve", xn[:], src_ap, rs, None, ALU.mult, None, src_keys + [rsk], [("xnb", j)])
        for q in range(4):
            bk, bkk = self.next_acc()
            for r in range(4):
                kc = q * 4 + r
                self.trm(bk[:, r * 128:(r + 1) * 128], xn[:, kc * 128:(kc + 1) * 128], [("xnb", j)], [bkk])
            nwb = self.cst[:, nw_lo + q * 4: nw_lo + q * 4 + 4].unsqueeze(2).to_broadcast([128, 4, 128])
            self.tt("dve", dstT[:, q * 4:(q + 1) * 4, i * 128:(i + 1) * 128],
                    bk[:].rearrange("p (a b) -> p a b", a=4), nwb, ALU.mult,
                    [bkk, ("cst", "nmix"), ("cst", "nffn")], [dst_key(i)])
        return rs, rsk

    def emit_stageA(self, kind, b):
        T = 512 if kind == "p" else self.TS
        NT = T // 128
        xsrc = self.x_p if kind == "p" else self.x_s
        t0 = b * 512
        hTk = lambda i: ("hT", i)
        for i in range(NT):
            sidx = self.ring("stage", self.NST)
            st = self.stage[sidx]
            self.dma("sp", st[:], xsrc[t0 + i * 128: t0 + (i + 1) * 128, :], [], [("stage", sidx)], f"st{sidx}")
            self.norm_transpose(st[:], [("stage", sidx)], 32, self.hT, hTk, i)

    def emit_block(self, ph, kind, b, do_A=True, hook=None):
        T = 512 if kind == "p" else self.TS
        NT = T // 128
        xsrc = self.x_p if kind == "p" else self.x_s
        ydst = self.y_p if kind == "p" else self.y_s
        t0 = b * 512
        if do_A:
            self.emit_stageA(kind, b)
        self.pending_reload = None
        if ph == 2:
            def reload():
                for i in range(NT):
                    self.dma("sp", self.xres[:, i, :], xsrc[t0 + i * 128: t0 + (i + 1) * 128, :], [],
                             [("xres", i, g) for g in range(4)], f"xr{i}")
            if getattr(self, "pending_combine", False):
                self.pending_reload = reload
            else:
                reload()
        hT_all = [("hT", i) for i in range(NT)]
        wtg, wk = self.wload("in", 0, KC, O_GR, 16)
        a, ak = self.next_acc()
        for kc in range(KC):
            self.mm(a[0:16, 0:T], wtg[:, kc, 0:16], self.hT[:, kc, 0:T], kc == 0, kc == KC - 1, [wk] + hT_all, [ak])
        self.s.op("dve", lambda e, a=a, T=T: e.tensor_copy(self.grT[:, 0:T], a[0:16, 0:T]), [ak], [("grT",)])
        groups = [
            dict(kind="hg", heads=[0, 1, 2, 3], vcol=O_HI, gcol=O_HGATE, qcol=O_HQ, kcol=O_HF, dv=128, hw=0, oc=0),
            dict(kind="hg", heads=[4, 5, 6, 7], vcol=O_HI + 512, gcol=O_HGATE + 512, qcol=O_HQ + 512, kcol=O_HF + 512, dv=128, hw=0, oc=512),
            dict(kind="gl", heads=[0, 1], vcol=O_GV, gcol=O_GGATE, qcol=O_GQ, kcol=O_GK, dv=256, hw=1, oc=1024),
            dict(kind="gl", heads=[2, 3], vcol=O_GV + 512, gcol=O_GGATE + 512, qcol=O_GQ + 256, kcol=O_GK + 256, dv=256, hw=1, oc=1536),
        ]
        for gi, g in enumerate(groups):
            self.emit_group(ph, kind, b, T, NT, g, gi, hT_all)
        if ph == 1:
            return
        import os
        dbg = os.environ.get("KDBG", "")
        if "noE" in dbg:
            return
        ogT_all = [("ogT", i) for i in range(NT)]
        for cg in range(4):
            wt, wk = self.wload("out", 0, KC, cg * 512, 512)
            for i in range(NT):
                a, ak = self.next_acc()
                for kc in range(KC):
                    self.mm(a[:], self.ogT[:, kc, i * 128:(i + 1) * 128], wt[:, kc, :], kc == 0, kc == KC - 1,
                            [wk, ("ogT", i)], [ak])
                xs = self.xres[:, i, cg * 512:(cg + 1) * 512]
                self.tt("dve", xs, a[:], xs, ALU.add, [ak, ("xres", i, cg)], [("xres", i, cg)])
        if "noF" in dbg:
            return
        for i in range(NT):
            self.norm_transpose(self.xres[:, i, :], [("xres", i, g_) for g_ in range(4)], 48, self.ogT,
                                lambda i_: ("ogT", i_), i)
        h2_all = [("ogT", i) for i in range(NT)]
        halves = [(0, 8), (8, 8), (16, 8), (24, 8), (32, 8), (40, 4)]
        for hidx_, (c0, nchk) in enumerate(halves):
            if hidx_ == 1 and hook is not None:
                hook()
            for fg in range(nchk // 4):
                cc0 = c0 + fg * 4
                wg, wgk = self.wload("gu", 0, KC, cc0 * 128, 512)
                wu, wuk = self.wload("gu", 0, KC, DFF + cc0 * 128, 512)
                for cb in range(4):
                    ag, agk = self.next_acc()
                    for kc in range(KC):
                        self.mm(ag[:, 0:T], wg[:, kc, cb * 128:(cb + 1) * 128], self.ogT[:, kc, 0:T], kc == 0,
                                kc == KC - 1, [wgk] + h2_all, [agk])
                    au, auk = self.next_acc()
                    for kc in range(KC):
                        self.mm(au[:, 0:T], wu[:, kc, cb * 128:(cb + 1) * 128], self.ogT[:, kc, 0:T], kc == 0,
                                kc == KC - 1, [wuk] + h2_all, [auk])
                    self.act(self.e2[:, 0:T], ag[:, 0:T], AF.Silu, [agk], [("e2",)])
                    lc = fg * 4 + cb
                    self.tt("dve", self.actT[:, lc, 0:T], self.e2[:, 0:T], au[:, 0:T], ALU.mult,
                            [("e2",), auk], [("actT", lc)])
            npieces = 1
            pk = nchk // npieces
            for cg in range(4):
                accs = [self.next_acc() for _ in range(NT)]
                for pc in range(npieces):
                    wd, wdk = self.wload("dn", (c0 + pc * pk) * 128, pk, cg * 512, 512)
                    for kk in range(pk):
                        lc = pc * pk + kk
                        for i in range(NT):
                            a, ak = accs[i]
                            self.mm(a[:], self.actT[:, lc, i * 128:(i + 1) * 128], wd[:, kk, :], lc == 0,
                                    lc == nchk - 1, [wdk, ("actT", lc)], [ak])
                for i in range(NT):
                    a, ak = accs[i]
                    xs = self.xres[:, i, cg * 512:(cg + 1) * 512]
                    self.tt("dve", xs, a[:], xs, ALU.add, [ak, ("xres", i, cg)], [("xres", i, cg)])
        if "noI" in dbg:
            return
        for i in range(NT):
            xk = [("xres", i, g_) for g_ in range(4)]
            j = self.ring("xnb", 2)
            ss, ssk = self.newstat()
            self.act(self.xnb[j][:], self.xres[:, i, :], AF.Square, xk, [("xnb", j), ssk], accum_out=ss)
            rt, rtk = self.newstat()
            self.act(rt, ss, AF.Ln, [ssk, ("cst", "eps")], [rtk], bias=self.cst[:, 68:69], scale=1.0 / D)
            rs, rsk = self.newstat()
            self.act(rs, rt, AF.Exp, [rtk], [rsk], scale=-0.5)
            self.stt(self.xres[:, i, :], self.xres[:, i, :], rs, self.nfin[:], ALU.mult, ALU.mult,
                     xk + [rsk, ("nfin",)], xk)
            ok = ("y", kind, b, i)
            self.dma("pool", ydst[t0 + i * 128: t0 + (i + 1) * 128, :], self.xres[:, i, :], xk, [ok] + xk, f"yout{i}")
            self.outkeys.add(ok)

    def emit_group(self, ph, kind, b, T, NT, g, gi, hT_all):
        dv = g["dv"]
        is_hg = g["kind"] == "hg"
        gs = 1.0 if is_hg else -1.0 / 16.0
        nch = T // CH
        heads = g["heads"]
        nh = len(heads)
        c = self.cst
        wv, wvk = self.wload("in", 0, KC, g["vcol"], 512)
        for i in range(NT):
            a, ak = self.next_acc()
            for kc in range(KC):
                self.mm(a[:], self.hT[:, kc, i * 128:(i + 1) * 128], wv[:, kc, :], kc == 0, kc == KC - 1,
                        [wvk, ("hT", i)], [ak])
            self.act(self.vtok[:, i, :], a[:], AF.Copy, [ak], [("vtok", i)])
        if ph == 2:
            wg, wgk = self.wload("in", 0, KC, g["gcol"], 512)
            for i in range(NT):
                a, ak = self.next_acc()
                for kc in range(KC):
                    self.mm(a[:], self.hT[:, kc, i * 128:(i + 1) * 128], wg[:, kc, :], kc == 0, kc == KC - 1,
                            [wgk, ("hT", i)], [ak])
                self.act(self.sgtok[:, i, :], a[:], AF.Silu, [ak], [("sgtok", i)])
                self.tt("pool", self.sgtok[:, i, :], self.sgtok[:, i, :], self.hnw[:, g["hw"], :], ALU.mult,
                        [("sgtok", i), ("hnw",)], [("sgtok", i)])
        ncolqk = 128 * nh
        if ph == 2:
            wq, wqk = self.wload("in", 0, KC, g["qcol"], ncolqk)
        wk_, wkk = self.wload("in", 0, KC, g["kcol"], ncolqk)
        def early(hi_, h):
            p = hi_ % 2
            fb, kb = self.fb[p], self.kb[p]
            fbk, kbk = ("fb", p), ("kb", p)
            a, ak = self.next_acc()
            for kc in range(KC):
                self.mm(a[:, 0:T], wk_[:, kc, hi_ * 128:(hi_ + 1) * 128], self.hT[:, kc, 0:T], kc == 0, kc == KC - 1,
                        [wkk] + hT_all, [ak])
            aqh = None
            if ph == 2:
                aq, aqk = self.next_acc()
                for kc in range(KC):
                    self.mm(aq[:, 0:T], wq[:, kc, hi_ * 128:(hi_ + 1) * 128], self.hT[:, kc, 0:T], kc == 0,
                            kc == KC - 1, [wqk] + hT_all, [aqk])
                aqh = (aq, aqk)
            if is_hg:
                self.act(fb[:, 0:T], a[:, 0:T], AF.Sigmoid, [ak], [fbk])
                self.ts("dve", fb[:, 0:T], fb[:, 0:T], c[:, 24 + h:25 + h], c[:, 16 + h:17 + h], ALU.mult,
                        ALU.add, [fbk, ("cst", "lb"), ("cst", "oml")], [fbk])
                self.ts("dve", kb[:, 0:T], fb[:, 0:T], -1.0, 1.0, ALU.mult, ALU.add, [fbk], [kbk])
                self.act(fb[:, 0:T], fb[:, 0:T], AF.Ln, [fbk], [fbk])
            else:
                self.act(kb[:, 0:T], a[:, 0:T], AF.Copy, [ak], [kbk])
                a2, a2k = self.next_acc()
                self.mm(a2[:, 0:T], self.wgk2[:, h * 128:(h + 1) * 128], self.grT[:, 0:T], True, True,
                        [("wgk2",), ("grT",)], [a2k])
                self.act(fb[:, 0:T], a2[:, 0:T], AF.Exp, [a2k, ("cst", "bgk")], [fbk],
                         bias=c[:, 64 + h:65 + h], scale=-1.0)
                self.act(fb[:, 0:T], fb[:, 0:T], AF.Ln, [fbk, ("cst", "one")], [fbk],
                         bias=c[:, 69:70], scale=1.0)
            return aqh

        def late(hi_, h, aqh):
            p = hi_ % 2
            fb, kb = self.fb[p], self.kb[p]
            fbk, kbk = ("fb", p), ("kb", p)
            hidx = h if is_hg else 8 + h
            qt, kt = self.qt[hi_], self.kt[hi_]
            Ach = self.Ach[:, hi_ * 8: hi_ * 8 + 8]
            Achk = ("Ach", hi_)
            self.s.op("dve", lambda e, T=T, fb=fb: e.tensor_tensor_scan(self.cb[:, 0:T], self.cst[:, 69:70].to_broadcast([128, T]),
                                                                         fb[:, 0:T], 0.0, ALU.mult, ALU.add),
                      [fbk, ("cst", "one")], [("cb",)])
            cb3 = self.cb[:, 0:T].rearrange("p (c t) -> p c t", t=CH)
            fb3 = fb[:, 0:T].rearrange("p (c t) -> p c t", t=CH)
            lastv = cb3[:, :, CH - 1]
            self.tt("dve", fb3, cb3, cb3[:, :, CH - 1:CH].to_broadcast([128, nch, CH]), ALU.subtract,
                    [("cb",)], [fbk])
            dl = self.dl
            self.s.op("dve", lambda e, lastv=lastv, dl=dl: e.tensor_copy(dl[:, 0:1], lastv[:, 0:1]), [("cb",)], [("dl",)])
            if nch > 1:
                self.tt("dve", dl[:, 1:nch], lastv[:, 1:nch], lastv[:, 0:nch - 1], ALU.subtract, [("cb",)], [("dl",)])
            self.act(Ach[:, 0:nch], dl[:, 0:nch], AF.Exp, [("dl",)], [Achk], scale=gs)
            if ph == 1:
                self.tt("dve", self.tot[:, hidx:hidx + 1], self.tot[:, hidx:hidx + 1], lastv[:, nch - 1:nch], ALU.add,
                        [("cb",), ("tot",)], [("tot",)])
            if ph == 2:
                self.act(self.e2[:, 0:T], fb[:, 0:T], AF.Exp, [fbk, ("cst", "lncq")], [("e2",)],
                         bias=c[:, 70:71], scale=gs)
            self.act(fb[:, 0:T], fb[:, 0:T], AF.Exp, [fbk], [fbk], scale=-gs)
            self.tt("pool", kt[:, 0:T], kb[:, 0:T], fb[:, 0:T], ALU.mult, [kbk, fbk], [("kt", hi_)])
            if ph == 1:
                self.trickle_casts(4)
            if ph == 2:
                aq, aqk = aqh
                self.act(self.qs[:, 0:T], aq[:, 0:T], AF.Silu if is_hg else AF.Copy, [aqk], [("qs",)])
                self.tt("pool", qt[:, 0:T], self.qs[:, 0:T], self.e2[:, 0:T], ALU.mult, [("qs",), ("e2",)], [("qt", hi_)])

        pend = None
        for hi_, h in enumerate(heads):
            aqh = early(hi_, h)
            if pend is not None:
                late(*pend)
            pend = (hi_, h, aqh)
        late(*pend)
        hpb = 512 // (2 * dv)

        def stage_P(i):
            tsl = slice(i * 128, (i + 1) * 128)
            bx, bxk = self.next_acc()
            for hi_ in range(nh):
                self.trm(bx[:, hi_ * 128:(hi_ + 1) * 128], self.kt[hi_][:, tsl], [("kt", hi_)], [bxk])
            if ph == 2:
                bs, bsk = self.next_acc()
                for hi_ in range(nh):
                    self.mm(bs[:, hi_ * 128:(hi_ + 1) * 128], self.kt[hi_][:, tsl], self.qt[hi_][:, tsl], True, True,
                            [("kt", hi_), ("qt", hi_)], [bsk])
            w = nh * 128
            self.act(self.KT0[0:64, 0:nh, :], bx[0:64, 0:w].rearrange("p (a b) -> p a b", a=nh), AF.Copy, [bxk], [("KT",)])
            self.act(self.KT1[64:128, 0:nh, :], bx[64:128, 0:w].rearrange("p (a b) -> p a b", a=nh), AF.Copy, [bxk],
                     [("KT",)])
            pr = None
            if ph == 2:
                pr = self.ring("PTs", 3)
                self.tt("dve", self.PTs[pr][:, 0:nh, :], bs[:, 0:w].rearrange("p (a b) -> p a b", a=nh),
                        self.mask[:].unsqueeze(1).to_broadcast([128, nh, 128]), ALU.mult, [bsk, ("mask",)], [("PTs", pr)])
            Us = {}
            bu = buk = None
            for hi_ in range(nh):
                vc = hi_ * dv
                if hi_ % hpb == 0:
                    bu, buk = self.next_acc()
                off = (hi_ % hpb) * 2 * dv
                for cc in range(2):
                    KTc = self.KT0 if cc == 0 else self.KT1
                    self.mm(bu[:, off + cc * dv: off + (cc + 1) * dv], KTc[:, hi_, :], self.vtok[:, i, vc:vc + dv],
                            True, True, [("KT",), ("vtok", i)], [buk])
                    Us[(hi_, cc)] = (bu[:, off + cc * dv: off + (cc + 1) * dv], buk)
            return (i, Us, pr)

        def stage_Q(i, Us, pr):
            spss = {}
            for hi_, h in enumerate(heads):
                hidx = h if is_hg else 8 + h
                scol = h * 128 if is_hg else 1024 + h * 256
                Achk = ("Ach", hi_)
                sps = []
                for cc in range(2):
                    cidx = i * 2 + cc
                    U, buk = Us[(hi_, cc)]
                    if kind == "p":
                        S_ap = self.Sbuf[:, scol:scol + dv]
                        Sk = ("S", hidx)
                    else:
                        sr = self.ring("Sr", 2)
                        S_ap = self.Sr[sr][:, 0:dv]
                        Sk = ("Sr", sr)
                        seq = cidx
                        src = self.st_hg[seq, h] if is_hg else self.st_gl[seq, h]
                        self.dma("sp", S_ap, src, [], [Sk], f"sr{sr}")
                    Acol = self.Ach[:, hi_ * 8 + cidx: hi_ * 8 + cidx + 1]
                    if ph == 2:
                        spr = self.ring("spbf", 8)
                        sp_ = self.spbf[spr][:, 0:dv]
                        self.act(sp_, S_ap, AF.Copy, [Sk, Achk], [("spbf", spr)], scale=Acol)
                        sps.append((sp_, ("spbf", spr)))
                    self.stt(S_ap, S_ap, Acol, U, ALU.mult, ALU.add, [Sk, Achk, buk], [Sk])
                    if kind == "s" and ph == 2:
                        dst = self.o_hg_s[seq, h] if is_hg else self.o_gl_s[seq, h]
                        ok = ("so", seq, hidx)
                        self.dma("pool", dst, S_ap, [Sk], [ok, Sk], f"sout{sr}")
                        self.outkeys.add(ok)
                spss[hi_] = sps
            if ph == 2:
                bo, bok = self.next_acc()
                for hi_ in range(nh):
                    vc = hi_ * dv
                    qt = self.qt[hi_]
                    self.mm(bo[:, vc:vc + dv], self.PTs[pr][:, hi_, :], self.vtok[:, i, vc:vc + dv], True, False,
                            [("PTs", pr), ("vtok", i)], [bok])
                    for cc in range(2):
                        sp_, spk = spss[hi_][cc]
                        p0 = cc * 64
                        self.mm(bo[p0:p0 + 64, vc:vc + dv], qt[:, i * 128 + p0: i * 128 + p0 + 64], sp_, False, True,
                                [("qt", hi_), spk], [bok])
                c0 = self.ring("hstat", 8) * 4
                hsb = self.hstat[:, c0:c0 + nh]
                hskeys = [("hstat", c0 + j) for j in range(nh)]
                for hi_ in range(nh):
                    vc = hi_ * dv
                    jk = self.ring("junk", 2)
                    self.act(self.junk[jk][:, 0:dv], bo[:, vc:vc + dv], AF.Square, [bok], [("junk", jk), hskeys[hi_]],
                             accum_out=self.hstat[:, c0 + hi_:c0 + hi_ + 1])
                self.act(hsb, hsb, AF.Ln, hskeys + [("cst", "eps")], hskeys, bias=c[:, 68:69], scale=1.0 / dv)
                self.act(hsb, hsb, AF.Exp, hskeys, hskeys, scale=-0.5)
                for hi_ in range(nh):
                    vc = hi_ * dv
                    self.stt(self.ogbuf[:, i, vc:vc + dv], bo[:, vc:vc + dv], self.hstat[:, c0 + hi_:c0 + hi_ + 1],
                             self.sgtok[:, i, vc:vc + dv], ALU.mult, ALU.mult,
                             [bok, hskeys[hi_], ("sgtok", i)], [("ogbuf", i)])

        pendP = None
        for i in range(NT):
            cur = stage_P(i)
            if ph == 2 and getattr(self, "pending_combine", False):
                self.pending_combine = False
                self.emit_exchange_combine()
                if self.pending_reload is not None:
                    self.pending_reload()
                    self.pending_reload = None
            if pendP is not None:
                stage_Q(*pendP)
            pendP = cur
        stage_Q(*pendP)
        if ph == 2:
            for i in range(NT):
                bk, bkk = self.next_acc()
                for rr in range(4):
                    self.trm(bk[:, rr * 128:(rr + 1) * 128], self.ogbuf[:, i, rr * 128:(rr + 1) * 128],
                             [("ogbuf", i)], [bkk])
                q = g["oc"] // 512
                self.act(self.ogT[:, q * 4:(q + 1) * 4, i * 128:(i + 1) * 128],
                         bk[:].rearrange("p (a b) -> p a b", a=4), AF.Copy, [bkk], [("ogT", i)])

    def emit_exchange(self):
        s = self.s
        self.trickle_casts(len(self.cast_q))
        Skeys = [("S", h) for h in range(12)]
        self.dma("pool", self.ag_src[:, :], self.Sbuf[:], Skeys, [("agsrc",)], "ag1")
        self.dma("pool", self.ag_src2[:, :], self.tot[:], [("tot",)], [("agsrc2",)], "ag2")
        s.op("pool", lambda e: e.collective_compute("AllGather", ALU.bypass, replica_groups=[[0, 1, 2, 3], [4, 5, 6, 7]],
                                                    ins=[self.ag_src], outs=[self.ag_dst]),
             [("agsrc",)], [("agdst",)], dsem="agcc", dinc=1)
        s.op("pool", lambda e: e.collective_compute("AllGather", ALU.bypass, replica_groups=[[0, 1, 2, 3], [4, 5, 6, 7]],
                                                    ins=[self.ag_src2], outs=[self.ag_dst2]),
             [("agsrc2",), ("agdst",)], [("agdst2",)], dsem="agcc2", dinc=1)
        self.dma("pool", self.ag_src3[:, :], self.tot[:], [("tot",)], [("agsrc3",)], "ag3")
        s.op("pool", lambda e: e.collective_compute("AllGather", ALU.bypass, replica_groups=[[0, 1, 2, 3], [4, 5, 6, 7]],
                                                    ins=[self.ag_src3], outs=[self.ag_dst3]),
             [("agsrc3",), ("agdst",), ("agdst2",)], [("agdst",), ("agdst2",), ("agdst3",)], dsem="agcc3", dinc=1)

    def emit_exchange_combine(self):
        s = self.s
        Skeys = [("S", h) for h in range(12)]
        R = self.xres[:, 0, :]
        L = self.xres[:, 1, :]
        Rk = [("xres", 0, g) for g in range(4)]
        Lk = [("xres", 1, g) for g in range(4)]
        c = self.cst
        s.op("dve", lambda e: e.memset(self.Sbuf[:], 0.0), [], Skeys)
        self.dma("sp", R, self.ag_dst[0:128, 0:D], [("agdst",)], Rk, "agl0")
        self.stt(self.Sbuf[:], R, c[:, 73:74], self.Sbuf[:], ALU.mult, ALU.add, Rk + Skeys + [("cst", "sel")], Skeys)
        for k in (1, 2):
            self.dma("sp", L, self.ag_dst[k * 128:(k + 1) * 128, 0:D], [("agdst",)], Lk, f"aglL{k}")
            self.dma("sp", self.tot[:, 0:16], self.ag_dst2[k * 128:(k + 1) * 128, 0:16], [("agdst2",)], [("tot",)],
                     f"aglT{k}")
            self.act(c[:, 76:84], self.tot[:, 0:8], AF.Exp, [("tot",)], [("cst", "atot")], scale=1.0)
            self.act(c[:, 84:88], self.tot[:, 8:12], AF.Exp, [("tot",)], [("cst", "atot")], scale=-1.0 / 16.0)
            for h in range(12):
                sc, dv = (h * 128, 128) if h < 8 else (1024 + (h - 8) * 256, 256)
                self.stt(R[:, sc:sc + dv], R[:, sc:sc + dv], c[:, 76 + h:77 + h], L[:, sc:sc + dv], ALU.mult, ALU.add,
                         Rk + Lk + [("cst", "atot")], Rk)
            self.stt(self.Sbuf[:], R, c[:, 73 + k:74 + k], self.Sbuf[:], ALU.mult, ALU.add,
                     Rk + Skeys + [("cst", "sel")], Skeys)

    def emit_prompt_state_out(self):
        Skeys = [("S", h) for h in range(12)]
        ok = ("spo",)
        self.dma("pool", self.o_hg_p.rearrange("h k v -> k h v"),
                 self.Sbuf[:, 0:1024].rearrange("k (h v) -> k h v", h=8), Skeys, [ok], "sout")
        self.dma("pool", self.o_gl_p.rearrange("h k v -> k h v"),
                 self.Sbuf[:, 1024:2048].rearrange("k (h v) -> k h v", h=4), Skeys, [ok], "sout")
        self.outkeys.add(ok)


def _build(nblk_p=8, nseq_s=4):
    k = Kern(nblk_p, nseq_s)
    return k


def make_inputs(inputs, nblk_p=8, nseq_s=4, ncores=8):
    f32 = np.float32
    xp = np.asarray(inputs["x_prompt"], f32)
    xs = np.asarray(inputs["x_sample"], f32)
    shg = np.asarray(inputs["state_hgrn"], f32)[0]
    sgl = np.asarray(inputs["state_gla"], f32)[0]
    TP = nblk_p * 512
    lbl = np.asarray(inputs["lb_logits"], f32)
    c_lb = np.ascontiguousarray(lbl.reshape(2, 8, 128).transpose(2, 0, 1).reshape(128, 16))
    feat = lambda v: np.ascontiguousarray(np.asarray(v, f32).reshape(-1, 128).T)
    c_nmix = feat(inputs["norm_mix"][0])
    c_nffn = feat(inputs["norm_ffn"][0])
    c_nfin = np.ascontiguousarray(np.broadcast_to(np.asarray(inputs["norm_final"], f32)[None, :], (128, D)))
    c_bgk = feat(inputs["b_gk"][0])
    hgn = np.asarray(inputs["hg_norm"], f32)[0]
    gln = np.asarray(inputs["gla_norm"], f32)[0]
    c_hnw = np.ascontiguousarray(np.broadcast_to(
        np.stack([np.tile(hgn, 4), np.tile(gln, 2)])[None], (128, 2, 512)))
    idx = np.arange(128)
    mask = ((idx[:, None] // 64 == idx[None, :] // 64) & (idx[:, None] <= idx[None, :])).astype(f32)
    ident = np.eye(128, dtype=f32).astype(ml_dtypes.bfloat16)
    common = dict(
        w_in=np.ascontiguousarray(np.asarray(inputs["w_in"], f32)[0]),
        w_out=np.ascontiguousarray(np.asarray(inputs["w_out"], f32)[0]),
        w_gu=np.ascontiguousarray(np.asarray(inputs["w_gate_up"], f32)[0]),
        w_dn=np.ascontiguousarray(np.asarray(inputs["w_down"], f32)[0]),
        w_gk2=np.ascontiguousarray(np.asarray(inputs["w_gk2"], f32)[0]),
        c_lb=c_lb, c_nmix=c_nmix, c_nffn=c_nffn, c_nfin=c_nfin, c_bgk=c_bgk, c_hnw=c_hnw,
        c_mask=mask, c_ident=ident,
    )
    maps = []
    for c in range(ncores):
        bseq, seg = c // 4, c % 4
        sel = np.zeros((128, 4), f32)
        sel[:, seg] = 1.0
        m = dict(common)
        m["x_p"] = np.ascontiguousarray(xp[bseq, seg * TP:(seg + 1) * TP])
        m["x_s"] = np.ascontiguousarray(xs[c * nseq_s:(c + 1) * nseq_s].reshape(nseq_s * 64, D))
        m["st_hg"] = np.ascontiguousarray(shg[c * nseq_s:(c + 1) * nseq_s])
        m["st_gl"] = np.ascontiguousarray(sgl[c * nseq_s:(c + 1) * nseq_s])
        m["c_sel"] = sel
        maps.append(m)
    return maps


_NC_CACHE = {}


def run(inputs, nblk_p=8, nseq_s=4):
    key = (nblk_p, nseq_s)
    kern = Kern(nblk_p, nseq_s)
    nc = kern.build()
    maps = make_inputs(inputs, nblk_p, nseq_s)
    res = run_bass_kernel_spmd(nc, maps, core_ids=list(range(8)))
    R = res.results
    TP = nblk_p * 512
    y_p = np.stack([np.concatenate([R[b * 4 + s]["y_p"] for s in range(4)], axis=0) for b in range(2)])
    y_s = np.concatenate([R[c]["y_s"].reshape(nseq_s, 64, D) for c in range(8)], axis=0)
    hg_p = np.stack([R[3]["o_hg_p"], R[7]["o_hg_p"]])[None]
    gl_p = np.stack([R[3]["o_gl_p"], R[7]["o_gl_p"]])[None]
    hg_s = np.concatenate([R[c]["o_hg_s"] for c in range(8)], axis=0)[None]
    gl_s = np.concatenate([R[c]["o_gl_s"] for c in range(8)], axis=0)[None]
    f = lambda a: np.ascontiguousarray(a, dtype=np.float32)
    return (f(y_p), f(y_s), f(hg_p), f(gl_p), f(hg_s), f(gl_s))


def kernel(**inputs):
    return run(inputs, 8, 4)
```
